# Optimizing a Trainium2 kernel written in Bass

```python
import jax, jax.numpy as jnp
from jax import lax
import numpy as np

D_MODEL = 1024
BATCH = 8
SEQ = 8192
DEPTH = 2
DEC_BATCH = 32
DEC_SEQ = 16
PAST_LEN = 2048

CHUNK = 64
D_FF = 2816
EPS = 1e-6
F_MIN = 1e-30
POOL_WINDOWS = (2, 4, 8, 16)
POOL_GROUPS = 4
POOL_GROUP_DIM = 64
POOL_DIM = POOL_GROUPS * POOL_GROUP_DIM
POOL_HIST = max(POOL_WINDOWS) - 1
LRU_BLOCKS = 4
LRU_BLOCK_DIM = 64
LRU_DIM = LRU_BLOCKS * LRU_BLOCK_DIM
CONV_WIDTH = 4
LRU_C = 8.0
HG_HEADS = 4
HG_KDIM = 128
HG_VDIM = 128
HG_FDIM = HG_HEADS * HG_KDIM
HG_IDIM = HG_HEADS * HG_VDIM
MIX_DIM = POOL_DIM + LRU_DIM + HG_IDIM
IN_DIM = POOL_DIM + 2 * LRU_DIM + 2 * HG_FDIM + 2 * HG_IDIM
SPLITS = [POOL_DIM, POOL_DIM + LRU_DIM, POOL_DIM + 2 * LRU_DIM,
          POOL_DIM + 2 * LRU_DIM + HG_FDIM, POOL_DIM + 2 * LRU_DIM + 2 * HG_FDIM,
          POOL_DIM + 2 * LRU_DIM + 2 * HG_FDIM + HG_IDIM]

kernel_name = 'hybrid_pool_rglru_hgrn2_stream_step'


def rms_norm(x, g):
    xf = x.astype(jnp.float32)
    y = xf * lax.rsqrt(jnp.mean(xf * xf, axis=-1, keepdims=True) + EPS)
    return (y * g.astype(jnp.float32)).astype(x.dtype)


def swiglu(x, w_gate, w_up, w_down):
    return (jax.nn.silu(x @ w_gate) * (x @ w_up)) @ w_down


def pool_mixer(u, hist, pos0, w, scale):
    B, T, _ = u.shape
    up = jnp.concatenate([hist.astype(u.dtype), u], axis=1)
    cs = jnp.cumsum(up.astype(jnp.float32), axis=1)
    cs = jnp.concatenate([jnp.zeros((B, 1, POOL_DIM), jnp.float32), cs], axis=1)
    pos = pos0 + jnp.arange(T)
    end = cs[:, POOL_HIST + 1:]
    means = []
    for gi, win in enumerate(POOL_WINDOWS):
        sl = slice(gi * POOL_GROUP_DIM, (gi + 1) * POOL_GROUP_DIM)
        start = cs[:, POOL_HIST + 1 - win:POOL_HIST + 1 - win + T, sl]
        cnt = jnp.minimum(win, pos + 1).astype(jnp.float32)[None, :, None]
        means.append((end[..., sl] - start) / cnt)
    pooled = (jnp.concatenate(means, axis=-1) - u.astype(jnp.float32)).astype(u.dtype)
    pg = pooled.reshape(B, T, POOL_GROUPS, POOL_GROUP_DIM)
    y = jnp.einsum('btgc,gcd->btgd', pg, w).reshape(B, T, POOL_DIM) * scale
    return y, up[:, -POOL_HIST:]


def rglru_mixer(xb, gb, conv_hist, h0, pos0, conv_w, conv_b, w_a, b_a, w_x, b_x, lam):
    B, T, _ = xb.shape
    xp = jnp.concatenate([conv_hist.astype(xb.dtype), xb], axis=1)
    conv = jnp.broadcast_to(conv_b, (B, T, LRU_DIM)).astype(xb.dtype)
    for k in range(CONV_WIDTH):
        conv = conv + xp[:, k:k + T] * conv_w[k]
    xc = conv.reshape(B, T, LRU_BLOCKS, LRU_BLOCK_DIM)
    r = jax.nn.sigmoid(jnp.einsum('btgc,gcd->btgd', xc, w_a).reshape(B, T, LRU_DIM) + b_a)
    i = jax.nn.sigmoid(jnp.einsum('btgc,gcd->btgd', xc, w_x).reshape(B, T, LRU_DIM) + b_x)
    log_a = -LRU_C * jax.nn.softplus(-lam.astype(jnp.float32)) * r.astype(jnp.float32)
    a = jnp.exp(log_a)
    mult = jnp.sqrt(jnp.maximum(-jnp.expm1(2.0 * log_a), 0.0))
    pos = pos0 + jnp.arange(T)
    mult = jnp.where((pos == 0)[None, :, None], 1.0, mult)
    bterm = mult * (i * conv).astype(jnp.float32)
    bterm = bterm.at[:, 0].add(a[:, 0] * h0.astype(jnp.float32))

    def combine(left, right):
        a1, b1 = left
        a2, b2 = right
        return a1 * a2, a2 * b1 + b2

    _, h = lax.associative_scan(combine, (a, bterm), axis=1)
    y = (h * jax.nn.gelu(gb.astype(jnp.float32))).astype(xb.dtype)
    return y, xp[:, -(CONV_WIDTH - 1):], h[:, -1].astype(h0.dtype)


def hgrn2_mixer(q, fz, v, g, S0, lb, norm_g):
    B, T, _ = q.shape
    lb = lb.astype(jnp.float32)
    zf = fz.astype(jnp.float32)
    f = lb + (1.0 - lb) * jax.nn.sigmoid(zf)
    log_f = jnp.log(jnp.maximum(f, F_MIN))
    k = (1.0 - lb) * jax.nn.sigmoid(-zf)
    qf = jax.nn.silu(q.astype(jnp.float32))
    vf = v.astype(jnp.float32)
    n_chunks = -(-T // CHUNK)
    pad = n_chunks * CHUNK - T

    def to_chunks(t, dh):
        t = jnp.pad(t, ((0, 0), (0, pad), (0, 0)))
        return t.reshape(B, n_chunks, CHUNK, HG_HEADS, dh).transpose(1, 0, 3, 2, 4)

    qc, kc, fc = to_chunks(qf, HG_KDIM), to_chunks(k, HG_KDIM), to_chunks(log_f, HG_KDIM)
    vc = to_chunks(vf, HG_VDIM)
    causal = jnp.tril(jnp.ones((CHUNK, CHUNK), bool))[:, :, None]

    def step(S, inp):
        qb, kb, lfb, vb = inp
        b = jnp.cumsum(lfb, axis=-2)
        diff = b[..., :, None, :] - b[..., None, :, :]
        decay = jnp.where(causal, jnp.exp(jnp.where(causal, diff, 0.0)), 0.0)
        att = jnp.einsum('bhtk,bhsk,bhtsk->bhts', qb, kb, decay)
        o = jnp.einsum('bhts,bhsv->bhtv', att, vb) + jnp.einsum('bhtk,bhkv->bhtv', qb * jnp.exp(b), S)
        b_last = b[..., -1:, :]
        S_new = jnp.exp(b_last[..., 0, :])[..., None] * S + jnp.einsum(
            'bhsk,bhsv->bhkv', kb * jnp.exp(b_last - b), vb)
        return S_new, o

    S, o = lax.scan(step, S0.astype(jnp.float32), (qc, kc, fc, vc))
    o = o.transpose(1, 0, 3, 2, 4).reshape(B, n_chunks * CHUNK, HG_HEADS, HG_VDIM)[:, :T]
    o = o * lax.rsqrt(jnp.mean(o * o, axis=-1, keepdims=True) + EPS)
    o = o * norm_g.astype(jnp.float32).reshape(HG_HEADS, HG_VDIM)
    o = o.reshape(B, T, HG_IDIM) * jax.nn.silu(g.astype(jnp.float32))
    return o.astype(q.dtype), S.astype(S0.dtype)


def layer(x, pos0, pool_hist, conv_hist, h0, S0, lb, w):
    h = rms_norm(x, w['ffn1_norm'])
    x = x + 0.5 * swiglu(h, w['ffn1_w_gate'], w['ffn1_w_up'], w['ffn1_w_down'])
    h = rms_norm(x, w['mix_norm'])
    z = h @ w['w_in']
    u_a, xb, gb, q, fz, v, g = jnp.split(z, SPLITS, axis=-1)
    ya, new_pool = pool_mixer(u_a, pool_hist, pos0, w['pool_w'], w['pool_scale'])
    yb, new_conv, new_h = rglru_mixer(xb, gb, conv_hist, h0, pos0, w['conv_w'], w['conv_b'],
                                      w['lru_w_a'], w['lru_b_a'], w['lru_w_x'], w['lru_b_x'], w['lru_lambda'])
    yc, new_S = hgrn2_mixer(q, fz, v, g, S0, lb, w['hgrn_norm'])
    x = x + jnp.concatenate([ya, yb, yc], axis=-1) @ w['w_out']
    h = rms_norm(x, w['ffn2_norm'])
    x = x + 0.5 * swiglu(h, w['ffn2_w_gate'], w['ffn2_w_up'], w['ffn2_w_down'])
    return x, new_pool, new_conv, new_h, new_S


def setup_inputs(seed: int = 0) -> dict:
    key = jax.random.key(seed)
    ks = jax.random.split(key, 32)
    f32 = jnp.float32
    L = DEPTH

    def nrm(k, shape, scale):
        return scale * jax.random.normal(k, shape, f32)

    u = jax.random.uniform(ks[21], (L, LRU_DIM), f32, 0.9, 0.999)
    s = u ** (1.0 / LRU_C)
    lru_lambda = jnp.log(s) - jnp.log1p(-s)
    return {
        'x_prompt': nrm(ks[0], (BATCH, SEQ, D_MODEL), 1.0),
        'x_sample': nrm(ks[1], (DEC_BATCH, DEC_SEQ, D_MODEL), 1.0),
        'state_pool': nrm(ks[2], (L, DEC_BATCH, POOL_HIST, POOL_DIM), 1.0),
        'state_conv': nrm(ks[3], (L, DEC_BATCH, CONV_WIDTH - 1, LRU_DIM), 1.0),
        'state_lru': nrm(ks[4], (L, DEC_BATCH, LRU_DIM), 0.5),
        'state_hgrn': nrm(ks[5], (L, DEC_BATCH, HG_HEADS, HG_KDIM, HG_VDIM), 0.5),
        'ffn1_norm': 1.0 + nrm(ks[6], (L, D_MODEL), 0.02),
        'ffn1_w_gate': nrm(ks[7], (L, D_MODEL, D_FF), D_MODEL ** -0.5),
        'ffn1_w_up': nrm(ks[8], (L, D_MODEL, D_FF), D_MODEL ** -0.5),
        'ffn1_w_down': nrm(ks[9], (L, D_FF, D_MODEL), D_FF ** -0.5),
        'mix_norm': 1.0 + nrm(ks[10], (L, D_MODEL), 0.02),
        'w_in': nrm(ks[11], (L, D_MODEL, IN_DIM), D_MODEL ** -0.5),
        'pool_w': nrm(ks[12], (L, POOL_GROUPS, POOL_GROUP_DIM, POOL_GROUP_DIM), POOL_GROUP_DIM ** -0.5),
        'pool_scale': 1.0 + nrm(ks[13], (L, POOL_DIM), 0.02),
        'conv_w': nrm(ks[14], (L, CONV_WIDTH, LRU_DIM), CONV_WIDTH ** -0.5),
        'conv_b': nrm(ks[15], (L, LRU_DIM), 0.01),
        'lru_w_a': nrm(ks[16], (L, LRU_BLOCKS, LRU_BLOCK_DIM, LRU_BLOCK_DIM), LRU_BLOCK_DIM ** -0.5),
        'lru_b_a': nrm(ks[17], (L, LRU_DIM), 0.01),
        'lru_w_x': nrm(ks[18], (L, LRU_BLOCKS, LRU_BLOCK_DIM, LRU_BLOCK_DIM), LRU_BLOCK_DIM ** -0.5),
        'lru_b_x': nrm(ks[19], (L, LRU_DIM), 0.01),
        'lru_lambda': lru_lambda,
        'hgrn_lb_logits': nrm(ks[20], (L, HG_FDIM), 1.0),
        'hgrn_norm': 1.0 + nrm(ks[22], (L, HG_IDIM), 0.02),
        'w_out': nrm(ks[23], (L, MIX_DIM, D_MODEL), MIX_DIM ** -0.5),
        'ffn2_norm': 1.0 + nrm(ks[24], (L, D_MODEL), 0.02),
        'ffn2_w_gate': nrm(ks[25], (L, D_MODEL, D_FF), D_MODEL ** -0.5),
        'ffn2_w_up': nrm(ks[26], (L, D_MODEL, D_FF), D_MODEL ** -0.5),
        'ffn2_w_down': nrm(ks[27], (L, D_FF, D_MODEL), D_FF ** -0.5),
        'final_norm': 1.0 + nrm(ks[28], (D_MODEL,), 0.02),
    }


def reference(x_prompt, x_sample, state_pool, state_conv, state_lru, state_hgrn,
              ffn1_norm, ffn1_w_gate, ffn1_w_up, ffn1_w_down, mix_norm, w_in,
              pool_w, pool_scale, conv_w, conv_b, lru_w_a, lru_b_a, lru_w_x, lru_b_x, lru_lambda,
              hgrn_lb_logits, hgrn_norm, w_out, ffn2_norm, ffn2_w_gate, ffn2_w_up, ffn2_w_down,
              final_norm):
    lb_p = jax.nn.softmax(hgrn_lb_logits.astype(jnp.float32), axis=0)
    lower_bounds = jnp.maximum(jnp.cumsum(lb_p, axis=0) - lb_p[0], 0.0)

    def params(l):
        return {'ffn1_norm': ffn1_norm[l], 'ffn1_w_gate': ffn1_w_gate[l], 'ffn1_w_up': ffn1_w_up[l],
                'ffn1_w_down': ffn1_w_down[l], 'mix_norm': mix_norm[l], 'w_in': w_in[l],
                'pool_w': pool_w[l], 'pool_scale': pool_scale[l], 'conv_w': conv_w[l], 'conv_b': conv_b[l],
                'lru_w_a': lru_w_a[l], 'lru_b_a': lru_b_a[l], 'lru_w_x': lru_w_x[l], 'lru_b_x': lru_b_x[l],
                'lru_lambda': lru_lambda[l], 'hgrn_norm': hgrn_norm[l], 'w_out': w_out[l],
                'ffn2_norm': ffn2_norm[l], 'ffn2_w_gate': ffn2_w_gate[l], 'ffn2_w_up': ffn2_w_up[l],
                'ffn2_w_down': ffn2_w_down[l]}

    def run(x, pos0, pool, conv, lru, hg):
        pools, convs, lrus, hgs = [], [], [], []
        for l in range(DEPTH):
            x, sp, sc, sl, sh = layer(x, pos0, pool[l], conv[l], lru[l], hg[l], lower_bounds[l], params(l))
            pools.append(sp)
            convs.append(sc)
            lrus.append(sl)
            hgs.append(sh)
        return (rms_norm(x, final_norm), jnp.stack(pools), jnp.stack(convs),
                jnp.stack(lrus), jnp.stack(hgs))

    bp = x_prompt.shape[0]
    dt = x_prompt.dtype
    zero_pool = jnp.zeros((DEPTH, bp, POOL_HIST, POOL_DIM), dt)
    zero_conv = jnp.zeros((DEPTH, bp, CONV_WIDTH - 1, LRU_DIM), dt)
    zero_lru = jnp.zeros((DEPTH, bp, LRU_DIM), dt)
    zero_hgrn = jnp.zeros((DEPTH, bp, HG_HEADS, HG_KDIM, HG_VDIM), dt)
    y_prompt, pool_p, conv_p, lru_p, hgrn_p = run(x_prompt, 0, zero_pool, zero_conv, zero_lru, zero_hgrn)
    y_sample, pool_s, conv_s, lru_s, hgrn_s = run(x_sample, PAST_LEN, state_pool, state_conv, state_lru, state_hgrn)
    return (y_prompt, y_sample, pool_p, conv_p, lru_p, hgrn_p, pool_s, conv_s, lru_s, hgrn_s)
```

```python
import numpy as np
import concourse.bass as bass
import concourse.mybir as mybir
from concourse.bass_utils import run_bass_kernel_spmd

F32 = mybir.dt.float32
BF16 = mybir.dt.bfloat16
AF = mybir.ActivationFunctionType
ALU = mybir.AluOpType

D = 1024
DFF = 2816
NK = 8
NJ = 22
NL = 2
TP = 512
NSEQ_S = 4
LS = 16
EPS = 1e-6
NSLOT = 6
SLOT = 4096
NCELL = 29
GELU_C = 0.7978845608028654

PIECES = []
for _i in range(11):
    PIECES.append(("gu1", 4096))
for _i in range(8):
    PIECES.append(("d1", 2816))
PIECES += [("A", 2048), ("B", 4096), ("Q", 4096), ("G", 4096), ("F", 4096), ("V", 4096), ("O", 4096), ("O", 4096)]
for _i in range(11):
    PIECES.append(("gu2", 4096))
for _i in range(8):
    PIECES.append(("d2", 2816))
NPL = len(PIECES)
POFF = []
_o = 0
for _l in range(NL):
    for _n, _s in PIECES:
        POFF.append((_o, _s))
        _o += _s
WTOT = _o

CV = {}
_c = 0


def _cv(name, w):
    global _c
    CV[name] = (_c, w)
    _c += w


for _l in range(NL):
    for _nm, _w in [("n1", 8), ("n2", 8), ("n3", 8), ("psc", 2), ("cw", 8), ("cb", 2), ("ba", 2), ("bx", 2),
                    ("lam", 2), ("hgn", 4)]:
        _cv(f"{_nm}{_l}", _w)
_cv("fn", 8)
_cv("lg0", 4)
_cv("lg1", 4)
_cv("invw", 2)
_cv("corr", 32)
NCV = _c
CM_ID, CM_CA, CM_RP, CM_RS, CM_BM = 0, 128, 192, 704, 768
NCM = 896


def _chunkmat(W, c0, ncols):
    return np.ascontiguousarray(W[:, c0:c0 + ncols].reshape(8, 128, ncols).transpose(1, 0, 2)).reshape(128, 8 * ncols)


def _vec(v):
    n = v.shape[0] // 128
    return np.ascontiguousarray(v.reshape(n, 128).T)


def pack_weights(inp):
    out = np.empty((128, WTOT), np.float32)
    pi = 0
    for l in range(NL):
        def put(a):
            nonlocal pi
            off, sz = POFF[pi]
            assert a.shape == (128, sz), (a.shape, sz)
            out[:, off:off + sz] = a
            pi += 1

        for tag in ("ffn1", "mix", "ffn2"):
            if tag == "mix":
                W = inp["w_in"][l]
                put(np.concatenate([_chunkmat(W, c * 128, 128) for c in range(0, 2)], axis=1))
                put(np.concatenate([_chunkmat(W, c * 128, 128) for c in range(2, 6)], axis=1))
                put(np.concatenate([_chunkmat(W, c * 128, 128) for c in range(6, 10)], axis=1))
                put(np.concatenate([_chunkmat(W, c * 128, 128) for c in range(18, 22)], axis=1))
                put(np.concatenate([_chunkmat(W, c * 128, 128) for c in range(10, 14)], axis=1))
                put(_chunkmat(W, 14 * 128, 512))
                Wo = inp["w_out"][l]
                for p in range(2):
                    put(np.concatenate([_chunkmat(Wo, c * 128, 128) for c in range(4 * p, 4 * p + 4)], axis=1))
            else:
                Wg, Wu, Wd = inp[f"{tag}_w_gate"][l], inp[f"{tag}_w_up"][l], inp[f"{tag}_w_down"][l]
                for jp in range(11):
                    parts = []
                    for jj in range(2):
                        j = 2 * jp + jj
                        parts.append(_chunkmat(Wg, j * 128, 128))
                        parts.append(_chunkmat(Wu, j * 128, 128))
                    put(np.concatenate(parts, axis=1))
                for c in range(8):
                    a = Wd[:, c * 128:(c + 1) * 128].reshape(22, 128, 128).transpose(1, 0, 2).reshape(128, 2816)
                    put(np.ascontiguousarray(a))
    assert pi == NL * NPL
    return out


def pack_consts(inp):
    cv = np.zeros((128, NCV), np.float32)

    def put(name, a):
        off, w = CV[name]
        cv[:, off:off + w] = a.reshape(128, w)

    for l in range(NL):
        put(f"n1{l}", _vec(inp["ffn1_norm"][l]))
        put(f"n2{l}", _vec(inp["mix_norm"][l]))
        put(f"n3{l}", _vec(inp["ffn2_norm"][l]))
        put(f"psc{l}", _vec(inp["pool_scale"][l]))
        put(f"cw{l}", np.ascontiguousarray(inp["conv_w"][l].reshape(4, 2, 128).transpose(2, 1, 0)))
        put(f"cb{l}", _vec(inp["conv_b"][l]))
        put(f"ba{l}", _vec(inp["lru_b_a"][l]))
        put(f"bx{l}", _vec(inp["lru_b_x"][l]))
        put(f"lam{l}", _vec(inp["lru_lambda"][l]))
        put(f"hgn{l}", _vec(inp["hgrn_norm"][l]))
    put("fn", _vec(inp["final_norm"]))
    put("lg0", _vec(inp["hgrn_lb_logits"][0]))
    put("lg1", _vec(inp["hgrn_lb_logits"][1]))
    wins = np.array([2, 4, 8, 16], np.float32)
    invw = np.zeros((128, 2), np.float32)
    corr = np.zeros((128, 2, 16), np.float32)
    for ch in range(2):
        for half in range(2):
            w = wins[2 * ch + half]
            invw[64 * half:64 * half + 64, ch] = 1.0 / w
            for pos in range(16):
                corr[64 * half:64 * half + 64, ch, pos] = w / min(w, pos + 1)
    put("invw", invw)
    put("corr", corr)
    cm = np.zeros((128, NCM), np.float32)
    cm[:, CM_ID:CM_ID + 128] = np.eye(128, dtype=np.float32)
    cm[:64, CM_CA:CM_CA + 64] = np.triu(np.ones((64, 64), np.float32))
    rp = np.ones(512, np.float32)
    rp[0::64] = 0.0
    rs = np.ones(64, np.float32)
    rs[0::16] = 0.0
    cm[:, CM_RP:CM_RP + 512] = rp[None]
    cm[:, CM_RS:CM_RS + 64] = rs[None]
    cm[0:64, CM_BM:CM_BM + 64] = np.triu(np.ones((64, 64), np.float32))
    cm[64:128, CM_BM + 64:CM_BM + 128] = np.triu(np.ones((64, 64), np.float32))
    bd = np.zeros((128, NL, 6, 128), np.float32)
    for l in range(NL):
        for mi, nm in enumerate(["pool_w", "lru_w_a", "lru_w_x"]):
            w = inp[nm][l]
            for ch in range(2):
                for half in range(2):
                    bd[64 * half:64 * half + 64, l, 2 * mi + ch, 64 * half:64 * half + 64] = w[2 * ch + half]
    return cv, cm, bd.reshape(128, NL * 6 * 128)


class Buf:
    __slots__ = ("w", "r")

    def __init__(self):
        self.w = None
        self.r = {}


class DSem:
    def __init__(self, sem):
        self.sem = sem
        self.val = 0


class Stream:
    pass


class KB:
    def __init__(self, NT, with_sample=True):
        self.NT = NT
        self.with_sample = with_sample
        nc = bass.Bass("TRN2", target_bir_lowering=False)
        self.nc = nc
        self.eng = dict(pe=nc.tensor, act=nc.scalar, dve=nc.vector, pool=nc.gpsimd, sp=nc.sync)
        self.esem = {k: nc.alloc_semaphore(name=f"es_{k}") for k in self.eng}
        self.ecnt = {k: 0 for k in self.eng}
        self.waited = {k: {} for k in self.eng}
        self.out_events = []
        self._ds_n = 0

    def dsem(self):
        self._ds_n += 1
        return DSem(self.nc.alloc_semaphore(name=f"ds{self._ds_n}"))

    def _wait(self, en, deps):
        w = self.waited[en]
        best = {}
        for sem, val in deps:
            k = sem.num
            if val > w.get(k, 0) and val > best.get(k, (None, 0))[1]:
                best[k] = (sem, val)
        for k, (sem, val) in best.items():
            self.eng[en].wait_ge(sem, val)
            w[k] = val

    def _deps(self, en, reads, writes):
        own = self.esem[en].num
        deps = []
        for b in reads:
            if b.w is not None:
                deps.append(b.w)
        for b in writes:
            if b.w is not None:
                deps.append(b.w)
            for ev in b.r.values():
                deps.append(ev)
        if en == "pe":
            deps = [d for d in deps if d[0].num != own]
        return deps

    def op(self, en, fn, reads=(), writes=()):
        self._wait(en, self._deps(en, reads, writes))
        ins = fn(self.eng[en])
        self.ecnt[en] += 1
        ev = (self.esem[en], self.ecnt[en])
        ins.then_inc(ev[0], 1)
        own = ev[0].num
        for b in writes:
            b.w = ev
            b.r = {}
        for b in reads:
            b.r[own] = ev
        return ev

    def mm(self, out_ap, outbuf, pairs, reads, pair_reads=None):
        self._wait("pe", self._deps("pe", reads, [outbuf]))
        n = len(pairs)
        ins = None
        for i, (lhsT, rhs) in enumerate(pairs):
            if pair_reads is not None:
                self._wait("pe", self._deps("pe", pair_reads[i], []))
            ins = self.nc.tensor.matmul(out_ap, lhsT=lhsT, rhs=rhs, start=(i == 0), stop=(i == n - 1))
        if pair_reads is not None:
            reads = list(reads) + [b for pr in pair_reads for b in pr]
        self.ecnt["pe"] += 1
        ev = (self.esem["pe"], self.ecnt["pe"])
        ins.then_inc(ev[0], 1)
        outbuf.w = ev
        outbuf.r = {}
        for b in reads:
            b.r[ev[0].num] = ev
        return ev

    def mm_kouter(self, st, w, wb, groups, rhs_t=None, rhs_b=None):
        self._wait("pe", self._deps("pe", [wb], [g[1] for g in groups]))
        evs = []
        rhs_t = st.h if rhs_t is None else rhs_t
        rhs_b = st.hb if rhs_b is None else rhs_b
        for k in range(NK):
            self._wait("pe", self._deps("pe", [rhs_b[k]], []))
            for out_ap, ob, c0 in groups:
                ins = self.nc.tensor.matmul(out_ap, lhsT=w[:, c0 + k * 128:c0 + (k + 1) * 128], rhs=rhs_t[:, k, :],
                                            start=(k == 0), stop=(k == NK - 1))
                if k == NK - 1:
                    self.ecnt["pe"] += 1
                    ev = (self.esem["pe"], self.ecnt["pe"])
                    ins.then_inc(ev[0], 1)
                    ob.w = ev
                    ob.r = {}
                    evs.append(ev)
        for b in [wb] + list(rhs_b):
            b.r[evs[-1][0].num] = evs[-1]

    def dma(self, q, out, in_, reads, writes, ds, is_output=False):
        deps = self._deps(q, reads, writes)
        if ds.val > 0:
            deps.append((ds.sem, ds.val))
        self._wait(q, deps)
        ins = self.eng[q].dma_start(out=out, in_=in_)
        ds.val += 16
        ev = (ds.sem, ds.val)
        ins.then_inc(ds.sem, 16)
        for b in writes:
            b.w = ev
            b.r = {}
        for b in reads:
            b.r[ds.sem.num] = ev
        if is_output:
            self.out_events.append(ev)
        return ev

    def act(self, out, in_, func, reads, writes, scale=None, bias=None):
        kw = {}
        if scale is not None:
            kw["scale"] = scale
        if bias is not None:
            kw["bias"] = bias
        return self.op("act", lambda e: e.activation(out=out, in_=in_, func=func, **kw), reads, writes)

    def tt(self, out, in0, in1, op, reads, writes, en="dve"):
        return self.op(en, lambda e: e.tensor_tensor(out=out, in0=in0, in1=in1, op=op), reads, writes)

    def stt(self, out, in0, scalar, in1, op0, op1, reads, writes):
        return self.op("dve", lambda e: e.scalar_tensor_tensor(out=out, in0=in0, scalar=scalar, in1=in1,
                                                               op0=op0, op1=op1), reads, writes)

    def ts(self, out, in0, s1, s2, op0, op1, reads, writes, en="dve"):
        if op1 is None:
            return self.op(en, lambda e: e.tensor_scalar(out=out, in0=in0, scalar1=s1, scalar2=None, op0=op0),
                           reads, writes)
        return self.op(en, lambda e: e.tensor_scalar(out=out, in0=in0, scalar1=s1, scalar2=s2, op0=op0, op1=op1),
                       reads, writes)

    def cp(self, out, in_, reads, writes, en="dve"):
        if en == "act":
            return self.op("act", lambda e: e.copy(out=out, in_=in_), reads, writes)
        return self.op(en, lambda e: e.tensor_copy(out=out, in_=in_), reads, writes)

    def cvap(self, name, i=0, n=1):
        off, w = CV[name]
        return self.cv[:, off + i:off + i + n]

    def setup(self):
        nc = self.nc
        NT = self.NT
        T = NT * TP
        self.T = T
        dt = nc.dram_tensor
        self.d_xp = dt("xp", [D, T], F32, kind="ExternalInput").ap()
        self.d_wpk = dt("wpk", [128, WTOT], F32, kind="ExternalInput").ap()
        self.d_cv = dt("cv", [128, NCV], F32, kind="ExternalInput").ap()
        self.d_cm = dt("cm", [128, NCM], F32, kind="ExternalInput").ap()
        self.d_bd = dt("bd", [128, NL * 768], F32, kind="ExternalInput").ap()
        self.d_wbf = dt("wbf", [128, WTOT], BF16, kind="Internal").ap()
        self.d_yp = dt("yp", [D, T], F32, kind="ExternalOutput").ap()
        self.d_poolP = dt("poolP", [NL, 128, 30], F32, kind="ExternalOutput").ap()
        self.d_convP = dt("convP", [NL, 128, 6], F32, kind="ExternalOutput").ap()
        self.d_lruP = dt("lruP", [NL, 128, 2], F32, kind="ExternalOutput").ap()
        self.d_hgP = dt("hgP", [NL, 128, 512], F32, kind="ExternalOutput").ap()
        if self.with_sample:
            self.d_xs = dt("xs", [D, 64], F32, kind="ExternalInput").ap()
            self.d_spool = dt("spool", [128, NL * 120], F32, kind="ExternalInput").ap()
            self.d_sconv = dt("sconv", [128, NL * 24], F32, kind="ExternalInput").ap()
            self.d_slru = dt("slru", [128, NL * 8], F32, kind="ExternalInput").ap()
            self.d_shg = dt("shg", [NL, NSEQ_S, 128, 512], F32, kind="ExternalInput").ap()
            self.d_ys = dt("ys", [D, 64], F32, kind="ExternalOutput").ap()
            self.d_poolS = dt("poolS", [128, NL * 120], F32, kind="ExternalOutput").ap()
            self.d_convS = dt("convS", [128, NL * 24], F32, kind="ExternalOutput").ap()
            self.d_lruS = dt("lruS", [128, NL * 8], F32, kind="ExternalOutput").ap()
            self.d_hgS = dt("hgS", [NL, NSEQ_S, 128, 512], F32, kind="ExternalOutput").ap()

        sb = nc.alloc_sbuf_tensor
        self.cv = sb("cv_sb", [128, NCV], F32)
        self.cm = sb("cm_sb", [128, NCM], F32)
        self.cvb = Buf()
        self.cmb = Buf()
        self.bdb = sb("bdb", [128, NL * 768], BF16)
        self.bdbb = Buf()
        self.ones = sb("ones", [128, 128], BF16)
        self.onesb = Buf()
        self.ident = sb("ident", [128, 128], BF16)
        self.identb = Buf()
        self.dv = sb("dv", [128, 32], F32)
        self.dvb = Buf()
        self.wring = sb("wring", [128, NSLOT, SLOT], BF16)
        self.wbuf = [Buf() for _ in range(NSLOT)]
        self.wsem = [self.dsem() for _ in range(NSLOT)]
        self.wbf_buf = [Buf() for _ in range(NL * NPL)]
        self.cast_sem = [self.dsem() for _ in range(8)]
        self.misc_sem = [self.dsem() for _ in range(4)]
        self._misc_i = 0
        self.ysem = [self.dsem() for _ in range(8)]
        self.xsem = [self.dsem() for _ in range(2)]
        self.ps = [nc.alloc_psum_tensor(f"ps{i}", [128, 512], F32) for i in range(7)]
        self.ps7 = nc.alloc_psum_tensor("ps7", [128, 1024], BF16)
        self.pb = [Buf() for _ in range(8)]
        self.psS_b = [Buf() for _ in range(8)]
        self.P = self.mk_stream("P", 1, TP, 64, True)
        self.S = self.mk_stream("S", NSEQ_S, LS, 16, False) if self.with_sample else None

    def misc(self):
        self._misc_i += 1
        return self.misc_sem[self._misc_i % 4]

    def mk_stream(self, name, nseq, Ls, C, isP):
        nc = self.nc
        sb = nc.alloc_sbuf_tensor
        st = Stream()
        st.name, st.nseq, st.Ls, st.C, st.isP = name, nseq, Ls, C, isP
        N = nseq * Ls
        st.N = N
        st.nch = N // C
        st.CW = ((max(N + 16, nseq * (Ls + 15)) + 7) // 8) * 8
        nx = 2 if isP else 1
        st.xres = [sb(f"xres{name}{i}", [128, NK, N], F32) for i in range(nx)]
        st.xb = [[Buf() for _ in range(NK)] for _ in range(nx)]
        st.h = sb(f"h{name}", [128, NK, N], BF16)
        st.hb = [Buf() for _ in range(NK)]
        st.ycat = sb(f"ycat{name}", [128, NK, N], BF16)
        st.yb = [Buf() for _ in range(NK)]
        st.rstd = sb(f"rstd{name}", [128, N], F32)
        st.rstdb = Buf()
        st.arena = sb(f"arena{name}", [128, NCELL, st.CW], F32)
        st.cellb = [Buf() for _ in range(NCELL)]
        st.UH = sb(f"UH{name}", [128, NL, 2, nseq, 15], F32)
        st.UHb = [Buf() for _ in range(NL)]
        st.XH = sb(f"XH{name}", [128, NL, 2, nseq, 3], F32)
        st.XHb = [Buf() for _ in range(NL)]
        st.HL = sb(f"HL{name}", [128, NL, 2, nseq], F32)
        st.HLb = [Buf() for _ in range(NL)]
        if isP:
            st.S32 = [[sb(f"S32{name}{l}", [128, 512], F32)] for l in range(NL)]
            st.S32b = [[Buf()] for l in range(NL)]
            st.S16 = [[sb(f"S16{name}{l}", [128, 512], BF16)] for l in range(NL)]
            st.S16b = [[Buf()] for l in range(NL)]
        else:
            t32 = [sb(f"S32{name}{s}", [128, 512], F32) for s in range(2)] * 2
            t16 = [sb(f"S16{name}{s}", [128, 512], BF16) for s in range(2)] * 2
            b32 = [Buf() for _ in range(2)] * 2
            b16 = [Buf() for _ in range(2)] * 2
            st.S32 = [t32 for l in range(NL)]
            st.S32b = [b32 for l in range(NL)]
            st.S16 = [t16 for l in range(NL)]
            st.S16b = [b16 for l in range(NL)]
        st.EBL = sb(f"EBL{name}", [128, 4, st.nch], F32)
        st.EBLb = Buf()
        if isP:
            st.VT = sb(f"VT{name}", [128, 4, 512], BF16)
            st.VTb = [Buf() for _ in range(4)]
            st.KTT = sb(f"KTT{name}", [128, 2, 512], BF16)
            st.KTTb = [Buf(), Buf()]
            st.ATT = sb(f"ATT{name}", [128, 2, 512], BF16)
            st.ATTb = [Buf(), Buf()]
        else:
            st.VT, st.VTb, st.KTT, st.KTTb, st.ATT, st.ATTb = (self.P.VT, self.P.VTb, self.P.KTT, self.P.KTTb,
                                                                 self.P.ATT, self.P.ATTb)
        st.O2 = sb(f"O2{name}", [128, N], BF16)
        st.O2b = Buf()
        if isP:
            bank = lambda i: (self.ps[i][:, 0:N], self.pb[i])
            st.psG = [bank(0), bank(1)]
            st.psU = [bank(2), bank(3)]
            st.psZ = [bank(0), bank(1), bank(2), bank(3)]
            st.psSS = bank(4)
            st.psATT = (self.ps[1], self.pb[1])
            st.psOC = [(self.ps[2][:, 0:256], self.pb[2]), (self.ps[2][:, 256:512], self.pb[2])]
        else:
            sub = lambda i: (self.ps[5][:, 64 * i:64 * i + 64], self.psS_b[0])
            st.psG = [sub(0), sub(1)]
            st.psU = [sub(2), sub(3)]
            st.psZ = [sub(0), sub(1), sub(2), sub(3)]
            st.psSS = sub(4)
            st.psATT = (self.ps[5][:, 320:384], self.psS_b[0])
            st.psOC = [sub(6), sub(7)]
        st.psVT = [(self.ps[6], self.pb[6]), (self.ps[0], self.pb[0])]
        st.psKTT = (self.ps7, self.pb[7])
        st.psKV = (self.ps[3], self.pb[3])
        return st

    def cf(self, st, i, a=0, n=None):
        n = st.N if n is None else n
        return st.arena[:, i, a:a + n]

    def cb16(self, st, i, half):
        return st.arena[:, i, :].bitcast(BF16)[:, half * st.CW: half * st.CW + st.N]

    def seg(self, st, ap):
        return ap.rearrange("p (s l) -> p s l", s=st.nseq)

    def start_weights(self):
        for p in range(NL * NPL):
            off, sz = POFF[p]
            self.dma("pool", self.d_wbf[:, off:off + sz], self.d_wpk[:, off:off + sz], [], [self.wbf_buf[p]],
                     self.cast_sem[p % 8])
        self.pc = 0
        self.pl = 0
        self.total_pieces = self.NT * NL * NPL
        for _ in range(NSLOT):
            self._load_next()

    def _load_next(self):
        g = self.pl
        if g >= self.total_pieces:
            return
        self.pl += 1
        p = g % (NL * NPL)
        off, sz = POFF[p]
        s = g % NSLOT
        self.dma("sp", self.wring[:, s, 0:sz], self.d_wbf[:, off:off + sz], [self.wbf_buf[p]], [self.wbuf[s]],
                 self.wsem[s])

    def take(self, expect):
        g = self.pc
        p = g % (NL * NPL)
        assert PIECES[p % NPL][0] == expect, (PIECES[p % NPL][0], expect)
        s = g % NSLOT
        return self.wring[:, s, :], self.wbuf[s]

    def release(self):
        self.pc += 1
        self._load_next()

    def prologue(self):
        nc = self.nc
        self.dma("sp", self.cv[:, :], self.d_cv[:, :], [], [self.cvb], self.misc())
        self.dma("sp", self.cm[:, :], self.d_cm[:, :], [], [self.cmb], self.misc())
        bd32 = self.P.arena[:, 0:3, 0:512]
        self.dma("sp", bd32, self.d_bd.rearrange("p (c w) -> p c w", c=3), [], self.P.cellb[0:3], self.misc())
        self.start_weights()
        self.op("dve", lambda e: e.memset(self.ones[:, :], 1.0), [], [self.onesb])
        self.cp(self.ident[:, :], self.cm[:, CM_ID:CM_ID + 128], [self.cmb], [self.identb])
        self.cp(self.bdb[:, :].rearrange("p (c w) -> p c w", c=3), bd32, self.P.cellb[0:3], [self.bdbb])
        dv = self.dv
        R, W = [self.cvb, self.dvb], [self.dvb]
        self.op("dve", lambda e: e.memset(dv[:, :], 0.0), [], W)
        self.tt(dv[:, 4:8], self.cvap("lg1", 0, 4), self.cvap("lg0", 0, 4), ALU.subtract, R, W)
        self.act(dv[:, 4:8], dv[:, 4:8], AF.Sigmoid, R, W)
        self.ts(dv[:, 8:16], dv[:, 0:8], -1.0, 1.0, ALU.mult, ALU.add, R, W)
        for l in range(NL):
            o = 16 + 2 * l
            self.act(dv[:, o:o + 2], self.cvap(f"lam{l}", 0, 2), AF.Exp, R, W, scale=-1.0)
            self.ts(dv[:, o:o + 2], dv[:, o:o + 2], 1.0, None, ALU.add, None, R, W)
            self.act(dv[:, o:o + 2], dv[:, o:o + 2], AF.Ln, R, W)
            self.ts(dv[:, o:o + 2], dv[:, o:o + 2], -8.0, None, ALU.mult, None, R, W)
        self.ts(dv[:, 20:24], dv[:, 16:20], 2.0, None, ALU.mult, None, R, W)
        self.ts(dv[:, 24:32], dv[:, 8:16], -1.0, None, ALU.mult, None, R, W)
        P = self.P
        for l in range(NL):
            self.op("dve", lambda e: e.memset(P.UH[:, l].rearrange("p c s j -> p (c s j)"), 0.0), [], [P.UHb[l]])
            self.op("dve", lambda e: e.memset(P.XH[:, l].rearrange("p c s j -> p (c s j)"), 0.0), [], [P.XHb[l]])
            self.op("dve", lambda e: e.memset(P.HL[:, l].rearrange("p c s -> p (c s)"), 0.0), [], [P.HLb[l]])
            self.op("dve", lambda e: e.memset(P.S32[l][0][:, :], 0.0), [], [P.S32b[l][0]])
            self.op("dve", lambda e: e.memset(P.S16[l][0][:, :], 0.0), [], [P.S16b[l][0]])
        if self.S is not None:
            S = self.S
            self.dma("sp", S.xres[0][:, :, :], self.d_xs.rearrange("(k p) t -> p k t", p=128), [], S.xb[0],
                     self.misc())
            self.dma("sp", S.UH[:].rearrange("p l c s j -> p (l c s j)"), self.d_spool[:, :], [], S.UHb, self.misc())
            self.dma("sp", S.XH[:].rearrange("p l c s j -> p (l c s j)"), self.d_sconv[:, :], [], S.XHb, self.misc())
            self.dma("sp", S.HL[:].rearrange("p l c s -> p (l c s)"), self.d_slru[:, :], [], S.HLb, self.misc())

    def load_x(self, i):
        P = self.P
        par = i % 2
        src = self.d_xp.rearrange("(k p) t -> p k t", p=128)[:, :, i * TP:(i + 1) * TP]
        self.dma("sp", P.xres[par][:, :, :], src, [], P.xb[par], self.xsem[par])

    def norm(self, st, xi, gname, final=False, presilu=False):
        x = st.xres[xi]
        xb = st.xb[xi]
        N = st.N
        for k in range(NK):
            self.act(self.cb16(st, 13 + k // 2, k % 2), x[:, k, :], AF.Square, [xb[k]], [st.cellb[13 + k // 2]])
        ss, ssb = st.psSS
        self.mm(ss, ssb, [(self.ones[:, :], self.cb16(st, 13 + k // 2, k % 2)) for k in range(NK)],
                [self.onesb], [[st.cellb[13 + k // 2]] for k in range(NK)])
        self.act(st.rstd[:, :], ss, AF.Ln, [ssb, self.epsb], [st.rstdb], scale=1.0 / D, bias=self.epsap)
        self.act(st.rstd[:, :], st.rstd[:, :], AF.Exp, [st.rstdb], [st.rstdb], scale=-0.5)
        if not final:
            for k in range(NK):
                self.stt(st.h[:, k, :], x[:, k, :], self.cvap(gname, k), st.rstd[:, :], ALU.mult, ALU.mult,
                         [xb[k], self.cvb, st.rstdb], [st.hb[k]])

    def ffn(self, streams, l, which):
        tag = "gu1" if which == 0 else "gu2"
        dtag = "d1" if which == 0 else "d2"
        gname = f"n1{l}" if which == 0 else f"n3{l}"
        for st in streams:
            self.norm(st, st.xi, gname, presilu=True)
        for jp in range(11):
            w, wb = self.take(tag)
            if jp == 0:
                for st in streams:
                    if not st.isP:
                        continue
                    groups = []
                    for jj in range(2):
                        groups.append((st.psG[jj][0], st.psG[jj][1], (2 * jj) * 1024))
                        groups.append((st.psU[jj][0], st.psU[jj][1], (2 * jj + 1) * 1024))
                    self.mm_kouter(st, w, wb, groups)
            for jj in range(2):
                j = 2 * jp + jj
                for st in streams:
                    G, Gb = st.psG[j % 2]
                    U, Ub = st.psU[j % 2]
                    g0 = (2 * jj) * 1024
                    u0 = (2 * jj + 1) * 1024
                    if jp > 0 or not st.isP:
                      self.mm(G, Gb, [(w[:, g0 + k * 128:g0 + (k + 1) * 128], st.h[:, k, :]) for k in range(NK)],
                            [wb], [[st.hb[k]] for k in range(NK)])
                      self.mm(U, Ub, [(w[:, u0 + k * 128:u0 + (k + 1) * 128], st.h[:, k, :]) for k in range(NK)],
                            [wb] + st.hb)
                    sgc = 11 + (j % 2)
                    self.act(self.cf(st, sgc), G, AF.Silu, [Gb], [st.cellb[sgc]])
                    self.tt(self.cb16(st, j // 2, j % 2), self.cf(st, sgc), U, ALU.mult, [st.cellb[sgc], Ub],
                            [st.cellb[j // 2]])
            self.release()
        for c in range(NK):
            w, wb = self.take(dtag)
            for st in streams:
                Dp, Db = st.psG[c % 2]
                self.mm(Dp, Db, [(w[:, j * 128:(j + 1) * 128], self.cb16(st, j // 2, j % 2)) for j in range(NJ)],
                        [wb] + st.cellb[0:11])
                x = st.xres[st.xi]
                self.stt(x[:, c, :], Dp, 0.5, x[:, c, :], ALU.mult, ALU.add, [Db, st.xb[st.xi][c]],
                         [st.xb[st.xi][c]])
            self.release()

    def zmm(self, st, w, wb, cl, zi):
        Z, Zb = st.psZ[zi % 4]
        self.mm(Z, Zb, [(w[:, (cl * 8 + k) * 128:(cl * 8 + k + 1) * 128], st.h[:, k, :]) for k in range(NK)],
                [wb], [[st.hb[k]] for k in range(NK)])
        return Z, Zb

    def mixer(self, streams, l, first):
        for st in streams:
            self.norm(st, st.xi, f"n2{l}")
        pe = self.pen
        w, wb = self.take("A")
        for st in streams:
            cb = st.cellb
            for ch in range(2):
                Z, Zb = self.zmm(st, w, wb, ch, ch)
                self.cp(self.pv3(st, ch, 0, 15), st.UH[:, l, ch], [st.UHb[l]], [cb[ch]], en=pe)
                self.cp(self.pv3(st, ch, 15, st.Ls), self.seg(st, Z), [Zb], [cb[ch]], en="act")
        self.release()
        w, wb = self.take("B")
        for st in streams:
            cb = st.cellb
            for ch in range(2):
                Z, Zb = self.zmm(st, w, wb, ch, ch)
                self.cp(self.lv3(st, 7 + ch, 0, 3), st.XH[:, l, ch], [st.XHb[l]], [cb[7 + ch]], en=pe)
                self.cp(self.lv3(st, 7 + ch, 3, st.Ls), self.seg(st, Z), [Zb], [cb[7 + ch]], en="act")
            for ch in range(2):
                Zg, Zgb = self.zmm(st, w, wb, 2 + ch, 2 + ch)
                self.act(self.cf(st, 9 + ch), Zg, AF.Gelu_apprx_tanh, [Zgb], [cb[9 + ch]])
        self.release()
        w, wb = self.take("Q")
        for st in streams:
            for h in range(4):
                Z, Zb = self.zmm(st, w, wb, h, h)
                self.act(self.cf(st, 23 + h), Z, AF.Silu, [Zb], [st.cellb[23 + h]])
        self.release()
        w, wb = self.take("G")
        for st in streams:
            for h in range(4):
                Z, Zb = self.zmm(st, w, wb, h, h)
                self.act(self.cb16(st, 27 + h // 2, h % 2), Z, AF.Silu, [Zb], [st.cellb[27 + h // 2]])
        self.release()
        w, wb = self.take("F")
        for st in streams:
            for h in range(4):
                Z, Zb = self.zmm(st, w, wb, h, h)
                self.act(self.cf(st, 19 + h), Z, AF.Sigmoid, [Zb], [st.cellb[19 + h]])
        self.release()
        w, wb = self.take("V")
        self.hg_vtok_P(self.P, w, wb)
        for st in streams:
            self.lru_gates(st, l)
        for st in streams:
            self.pool_chain(st, l, first and st.isP)
        for st in streams:
            self.lru_chain(st, l, first and st.isP)
        for st in streams:
            self.hg_prep(st, l)
        for st in streams:
            if st.isP:
                self.hg_chunks_P(st, l)
            else:
                self.hg_chunks(st, l, w, wb)
        self.release()
        for st in streams:
            self.hg_out(st, l)
        for p in range(2):
            w, wb = self.take("O")
            if p == 0:
                P_ = self.P
                self.mm_kouter(P_, w, wb, [(P_.psZ[cl][0], P_.psZ[cl][1], cl * 1024) for cl in range(4)],
                               rhs_t=P_.ycat, rhs_b=P_.yb)
            for cl in range(4):
                c = 4 * p + cl
                for st in streams:
                    Z, Zb = st.psZ[c % 4]
                    if p > 0 or not st.isP:
                      self.mm(Z, Zb, [(w[:, (cl * 8 + k) * 128:(cl * 8 + k + 1) * 128], st.ycat[:, k, :])
                                    for k in range(NK)], [wb], [[st.yb[k]] for k in range(NK)])
                    x = st.xres[st.xi]
                    self.tt(x[:, c, :], Z, x[:, c, :], ALU.add, [Zb, st.xb[st.xi][c]], [st.xb[st.xi][c]])
            self.release()

    def pv3(self, st, cell, a, n):
        LW = 15 + st.Ls
        return st.arena[:, cell, 0:st.nseq * LW].rearrange("p (s l) -> p s l", s=st.nseq)[:, :, a:a + n]

    def lv3(self, st, cell, a, n):
        LW = 3 + st.Ls
        return st.arena[:, cell, 0:st.nseq * LW].rearrange("p (s l) -> p s l", s=st.nseq)[:, :, a:a + n]

    def pool_chain(self, st, l, first):
        Ls = st.Ls
        HP = 15
        LW = HP + Ls
        cb = st.cellb
        pe = self.pen
        v3 = lambda cell, a, n: self.pv3(st, cell, a, n)
        for ch in range(2):
            ub = ch
            self.tt(v3(2, 1, LW - 1), v3(ub, 1, LW - 1), v3(ub, 0, LW - 1), ALU.add, [cb[ub]], [cb[2]], en=pe)
            self.tt(v3(3, 3, LW - 3), v3(2, 3, LW - 3), v3(2, 1, LW - 3), ALU.add, [cb[2]], [cb[3]], en=pe)
            if ch == 1:
                self.tt(v3(4, 7, LW - 7), v3(3, 7, LW - 7), v3(3, 3, LW - 7), ALU.add, [cb[3]], [cb[4]], en=pe)
                self.tt(v3(5, 15, LW - 15), v3(4, 15, LW - 15), v3(4, 7, LW - 15), ALU.add, [cb[4]], [cb[5]], en=pe)
                lo, hi = 4, 5
            else:
                lo, hi = 2, 3
            if first:
                corr = self.cv[:, CV["corr"][0] + 16 * ch: CV["corr"][0] + 16 * ch + 16]
                for cell, p0 in ((lo, 0), (hi, 64)):
                    a = st.arena[p0:p0 + 64, cell, HP:HP + 16]
                    self.tt(a, a, corr[p0:p0 + 64, :], ALU.mult, [cb[cell], self.cvb], [cb[cell]], en=pe)
            pl = self.seg(st, self.cb16(st, 6, ch))
            for cell, p0 in ((lo, 0), (hi, 64)):
                self.stt(pl[p0:p0 + 64], v3(cell, HP, Ls)[p0:p0 + 64], self.cvap("invw", ch)[p0:p0 + 64],
                         v3(ub, HP, Ls)[p0:p0 + 64], ALU.mult, ALU.subtract, [cb[cell], cb[ub], self.cvb], [cb[6]])
            self.cp(st.UH[:, l, ch], v3(ub, Ls, HP), [cb[ub]], [st.UHb[l]], en=pe)
            Y, Yb = st.psZ[2 + ch]
            self.mm(Y, Yb, [(self.bdb[:, (l * 6 + ch) * 128:(l * 6 + ch + 1) * 128], self.cb16(st, 6, ch))],
                    [self.bdbb, cb[6]])
            self.act(st.ycat[:, ch, :], Y, AF.Identity, [Yb, self.cvb], [st.yb[ch]], scale=self.cvap(f"psc{l}", ch))

    def lru_gates(self, st, l):
        Ls = st.Ls
        cb = st.cellb
        c3 = lambda cell: self.seg(st, self.cf(st, cell))
        for ch in range(2):
            xbc = 7 + ch
            cvc = 11 + ch
            cw = CV[f"cw{l}"][0] + 4 * ch
            self.ts(c3(cvc), self.lv3(st, xbc, 0, Ls), self.cv[:, cw:cw + 1], self.cvap(f"cb{l}", ch), ALU.mult,
                    ALU.add, [cb[xbc], self.cvb], [cb[cvc]])
            for k in range(1, 4):
                self.stt(c3(cvc), self.lv3(st, xbc, k, Ls), self.cv[:, cw + k:cw + k + 1], c3(cvc), ALU.mult, ALU.add,
                         [cb[xbc], self.cvb, cb[cvc]], [cb[cvc]])
            self.cp(st.XH[:, l, ch], self.lv3(st, xbc, Ls, 3), [cb[xbc]], [st.XHb[l]], en=self.pen)
            cvb16 = self.cb16(st, 17, ch)
            self.cp(cvb16, self.cf(st, cvc), [cb[cvc]], [cb[17]], en="act")
            Rp, Rb = st.psZ[ch]
            self.mm(Rp, Rb, [(self.bdb[:, (l * 6 + 2 + ch) * 128:(l * 6 + 3 + ch) * 128], cvb16)], [self.bdbb, cb[17]])
            self.act(self.cf(st, 13 + ch), Rp, AF.Sigmoid, [Rb, self.cvb], [cb[13 + ch]], bias=self.cvap(f"ba{l}", ch))
            Ip, Ib = st.psZ[2 + ch]
            self.mm(Ip, Ib, [(self.bdb[:, (l * 6 + 4 + ch) * 128:(l * 6 + 5 + ch) * 128], cvb16)], [self.bdbb, cb[17]])
            self.act(self.cf(st, 15 + ch), Ip, AF.Sigmoid, [Ib, self.cvb], [cb[15 + ch]], bias=self.cvap(f"bx{l}", ch))

    def lru_chain(self, st, l, first):
        Ls, ns = st.Ls, st.nseq
        cb = st.cellb
        c3 = lambda cell: self.seg(st, self.cf(st, cell))
        CH = (0, 1)
        cvc = lambda ch: 11 + ch
        rc = lambda ch: 13 + ch
        ic = lambda ch: 15 + ch
        ac = lambda ch: 2 + ch
        mc = lambda ch: 4 + ch
        hc = lambda ch: (18, 0)[ch]
        for ch in CH:
            cl = self.dv[:, 16 + 2 * l + ch:16 + 2 * l + ch + 1]
            cl2 = self.dv[:, 20 + 2 * l + ch:20 + 2 * l + ch + 1]
            self.act(self.cf(st, ac(ch)), self.cf(st, rc(ch)), AF.Exp, [cb[rc(ch)], self.dvb], [cb[ac(ch)]], scale=cl)
            self.act(self.cf(st, mc(ch)), self.cf(st, rc(ch)), AF.Exp, [cb[rc(ch)], self.dvb], [cb[mc(ch)]], scale=cl2)
        for ch in CH:
            self.tt(self.cf(st, ic(ch)), self.cf(st, ic(ch)), self.cf(st, cvc(ch)), ALU.mult, [cb[ic(ch)], cb[cvc(ch)]],
                    [cb[ic(ch)]], en=self.pen)
            self.ts(self.cf(st, mc(ch)), self.cf(st, mc(ch)), 0.99999994, None, ALU.min, None, [cb[mc(ch)]], [cb[mc(ch)]])
        for ch in CH:
            self.act(self.cf(st, mc(ch)), self.cf(st, mc(ch)), AF.Ln, [cb[mc(ch)]], [cb[mc(ch)]], scale=-1.0, bias=self.oneap)
        for ch in CH:
            self.act(self.cf(st, mc(ch)), self.cf(st, mc(ch)), AF.Exp, [cb[mc(ch)]], [cb[mc(ch)]], scale=0.5)
            if first:
                self.op("dve", lambda e: e.memset(st.arena[:, mc(ch), 0:1], 1.0), [], [cb[mc(ch)]])
        for ch in CH:
            self.tt(self.cf(st, ic(ch)), self.cf(st, ic(ch)), self.cf(st, mc(ch)), ALU.mult, [cb[ic(ch)], cb[mc(ch)]],
                    [cb[ic(ch)]])
        for ch in CH:
            for s in range(ns):
                sl = slice(s * Ls, (s + 1) * Ls)
                self.op("dve", lambda e: e.tensor_tensor_scan(
                    out=st.arena[:, hc(ch), sl], data0=st.arena[:, ac(ch), sl], data1=st.arena[:, ic(ch), sl],
                    initial=st.HL[:, l, ch, s:s + 1], op0=ALU.mult, op1=ALU.add),
                    [cb[ac(ch)], cb[ic(ch)], st.HLb[l]], [cb[hc(ch)]])
        for ch in CH:
            self.cp(st.HL[:, l, ch, :], c3(hc(ch))[:, :, Ls - 1], [cb[hc(ch)]], [st.HLb[l]])
            self.tt(st.ycat[:, 2 + ch, :], self.cf(st, hc(ch)), self.cf(st, 9 + ch), ALU.mult, [cb[hc(ch)], cb[9 + ch]],
                    [st.yb[2 + ch]], en=self.pen)

    def hg_prep(self, st, l):
        cb = st.cellb
        C, nch = st.C, st.nch
        pe = self.pen
        rm = self.cm[:, CM_RP:CM_RP + 512] if st.isP else self.cm[:, CM_RS:CM_RS + 64]
        H = range(4)
        cells = [(6, 7), (8, 17), (11, 13), (12, 14)]
        LF = lambda h: cells[h][0]
        B = lambda h: cells[h][1]
        sg = lambda h: 19 + h
        lb = lambda h: self.dv[:, 4 * l + h:4 * l + h + 1]
        oml = lambda h: self.dv[:, 8 + 4 * l + h:8 + 4 * l + h + 1]
        noml = lambda h: self.dv[:, 24 + 4 * l + h:24 + 4 * l + h + 1]
        for h in H:
            self.act(self.cf(st, LF(h)), self.cf(st, sg(h)), AF.Ln, [cb[sg(h)], self.dvb], [cb[LF(h)]],
                     scale=oml(h), bias=lb(h))
        for h in H:
            self.op("dve", lambda e: e.tensor_tensor_scan(out=self.cf(st, B(h)), data0=rm[:, 0:st.N],
                                                          data1=self.cf(st, LF(h)), initial=0.0,
                                                          op0=ALU.mult, op1=ALU.add),
                    [self.cmb, cb[LF(h)]], [cb[B(h)]])
            self.act(self.cf(st, sg(h)), self.cf(st, sg(h)), AF.Identity, [cb[sg(h)], self.dvb], [cb[sg(h)]],
                     scale=noml(h), bias=oml(h))
        for h in H:
            self.act(self.cf(st, LF(h)), self.cf(st, B(h)), AF.Exp, [cb[B(h)]], [cb[LF(h)]])
        for h in H:
            self.act(self.cf(st, B(h)), self.cf(st, B(h)), AF.Exp, [cb[B(h)]], [cb[B(h)]], scale=-1.0)
        for h in H:
            ebl_src = self.cf(st, LF(h)).rearrange("p (c j) -> p c j", j=C)[:, :, C - 1]
            self.cp(st.EBL[:, h, :], ebl_src, [cb[LF(h)]], [st.EBLb])
            self.tt(self.cb16(st, h // 2, h % 2), self.cf(st, 23 + h), self.cf(st, LF(h)), ALU.mult,
                    [cb[23 + h], cb[LF(h)]], [cb[h // 2]])
            self.tt(self.cb16(st, 2 + h // 2, h % 2), self.cf(st, sg(h)), self.cf(st, B(h)), ALU.mult,
                    [cb[sg(h)], cb[B(h)]], [cb[2 + h // 2]], en=pe)

    def hg_vtok_P(self, st, w, wb):
        for blk in range(4):
            VTp, VTpb = st.psVT[blk % 2]
            cols = slice(blk * 128, (blk + 1) * 128)
            self.mm(VTp[:, :], VTpb, [(st.h[:, k, cols], w[:, k * 512:(k + 1) * 512]) for k in range(NK)],
                    [wb] + st.hb)
            self.cp(st.VT[:, blk, :], VTp[:, :], [VTpb], [st.VTb[blk]], en="act")

    def hg_chunks_P(self, st, l):
        cb = st.cellb
        bm = self.cm[:, CM_BM:CM_BM + 128]
        S32, S32b = st.S32[l][0], st.S32b[l][0]
        S16, S16b = st.S16[l][0], st.S16b[l][0]

        def front(blk):
            par = blk % 2
            c128 = slice(blk * 128, (blk + 1) * 128)
            KTp, KTpb = st.psKTT
            self._wait("pe", self._deps("pe", [cb[2], cb[3], self.identb], [KTpb]))
            ins = None
            for h in range(4):
                ins = self.nc.tensor.transpose(out=KTp[:, h * 128:(h + 1) * 128],
                                               in_=self.cb16(st, 2 + h // 2, h % 2)[:, c128],
                                               identity=self.ident[:, :])
            self.ecnt["pe"] += 1
            ev = (self.esem["pe"], self.ecnt["pe"])
            ins.then_inc(ev[0], 1)
            KTpb.w = ev
            KTpb.r = {}
            for b_ in (cb[2], cb[3], self.identb):
                b_.r[ev[0].num] = ev
            self.cp(st.KTT[:, par, :], KTp[:, 0:512], [KTpb], [st.KTTb[par]])
            ATp, ATpb = st.psATT
            for h in range(4):
                self.mm(ATp[:, h * 128:(h + 1) * 128], ATpb,
                        [(self.cb16(st, 2 + h // 2, h % 2)[:, c128], self.cb16(st, h // 2, h % 2)[:, c128])],
                        [cb[2], cb[3], cb[0], cb[1]])
            self.tt(st.ATT[:, par, :].rearrange("p (h t) -> p h t", h=4),
                    ATp[:, :].rearrange("p (h t) -> p h t", h=4), bm.unsqueeze(1).broadcast_to([128, 4, 128]),
                    ALU.mult, [ATpb, self.cmb], [st.ATTb[par]])

        def back(blk):
            par = blk % 2
            for sub in range(2):
                c = 2 * blk + sub
                p0 = 64 * sub
                cols = slice(c * 64, (c + 1) * 64)
                OCp, OCpb = st.psOC[c % 2]
                for h in range(4):
                    self.mm(OCp[:, h * 64:(h + 1) * 64], OCpb,
                            [(st.VT[p0:p0 + 64, blk, h * 128:(h + 1) * 128],
                              st.ATT[p0:p0 + 64, par, h * 128 + p0:h * 128 + p0 + 64]),
                             (S16[:, h * 128:(h + 1) * 128], self.cb16(st, h // 2, h % 2)[:, cols])],
                            [st.VTb[blk], st.ATTb[par], S16b, cb[0], cb[1]])
                self.cp(st.arena[:, 19:23, c * 64:(c + 1) * 64], OCp[:, 0:256].rearrange("p (h t) -> p h t", h=4),
                        [OCpb], cb[19:23], en="act")
                KVp, KVpb = st.psKV
                for h in range(4):
                    self.mm(KVp[:, h * 128:(h + 1) * 128], KVpb,
                            [(st.KTT[p0:p0 + 64, par, h * 128:(h + 1) * 128], st.VT[p0:p0 + 64, blk, h * 128:(h + 1) * 128])],
                            [st.KTTb[par], st.VTb[blk]])
                s3 = S32[:, :].rearrange("p (h v) -> p h v", h=4)
                self.tt(S32[:, :], S32[:, :], KVp[:, :], ALU.add, [S32b, KVpb], [S32b])
                self.tt(s3, s3, st.EBL[:, :, c:c + 1].broadcast_to([128, 4, 128]), ALU.mult, [S32b, st.EBLb], [S32b])
                self.cp(S16[:, :], S32[:, :], [S32b], [S16b], en="act")

        front(0)
        for blk in range(4):
            if blk + 1 < 4:
                front(blk + 1)
            back(blk)

    def hg_chunks(self, st, l, w, wb):
        cb = st.cellb
        C, nch = st.C, st.nch
        ca = self.cm[0:C, CM_CA:CM_CA + C]
        for c in range(nch):
            par = c % 2
            cols = slice(c * C, (c + 1) * C)
            sidx = 0 if st.isP else c % 2
            S32, S32b = st.S32[l][sidx], st.S32b[l][sidx]
            S16, S16b = st.S16[l][sidx], st.S16b[l][sidx]
            if not st.isP:
                self.dma("sp", S32[:, :], self.d_shg[l, c], [], [S32b], self.misc())
                self.cp(S16[:, :], S32[:, :], [S32b], [S16b], en="act")
            VTp, VTpb = st.psVT[par]
            self.mm(VTp[0:C, :], VTpb, [(st.h[:, k, cols], w[:, k * 512:(k + 1) * 512]) for k in range(NK)],
                    [wb] + st.hb)
            self.cp(st.VT[0:C, par, :], VTp[0:C, :], [VTpb], [st.VTb[par]], en="act")
            KTp, KTpb = st.psKTT
            self._wait("pe", self._deps("pe", [cb[2], cb[3], self.identb], [KTpb]))
            ins = None
            for h in range(4):
                ins = self.nc.tensor.transpose(out=KTp[0:C, h * 128:(h + 1) * 128],
                                               in_=self.cb16(st, 2 + h // 2, h % 2)[:, cols],
                                               identity=self.ident[:, :])
            self.ecnt["pe"] += 1
            ev = (self.esem["pe"], self.ecnt["pe"])
            ins.then_inc(ev[0], 1)
            KTpb.w = ev
            KTpb.r = {}
            for b in (cb[2], cb[3], self.identb):
                b.r[ev[0].num] = ev
            self.cp(st.KTT[0:C, par, :], KTp[0:C, 0:512], [KTpb], [st.KTTb[par]])
            ATp, ATpb = st.psATT
            for h in range(4):
                self.mm(ATp[0:C, h * C:(h + 1) * C], ATpb,
                        [(self.cb16(st, 2 + h // 2, h % 2)[:, cols], self.cb16(st, h // 2, h % 2)[:, cols])],
                        [cb[2], cb[3], cb[0], cb[1]])
            att_out = st.ATT[0:C, par, 0:4 * C].rearrange("p (h t) -> p h t", h=4)
            att_in = ATp[0:C, 0:4 * C].rearrange("p (h t) -> p h t", h=4)
            self.tt(att_out, att_in, ca.unsqueeze(1).broadcast_to([C, 4, C]), ALU.mult, [ATpb, self.cmb],
                    [st.ATTb[par]])
            OCp, OCpb = st.psOC[par]
            for h in range(4):
                self.mm(OCp[:, h * C:(h + 1) * C], OCpb,
                        [(st.VT[0:C, par, h * 128:(h + 1) * 128], st.ATT[0:C, par, h * C:(h + 1) * C]),
                         (S16[:, h * 128:(h + 1) * 128], self.cb16(st, h // 2, h % 2)[:, cols])],
                        [st.VTb[par], st.ATTb[par], S16b, cb[0], cb[1]])
            os_out = st.arena[:, 19:23, c * C:(c + 1) * C]
            self.cp(os_out, OCp[:, 0:4 * C].rearrange("p (h t) -> p h t", h=4), [OCpb], cb[19:23], en="act")
            KVp, KVpb = st.psKV
            for h in range(4):
                self.mm(KVp[:, h * 128:(h + 1) * 128], KVpb,
                        [(st.KTT[0:C, par, h * 128:(h + 1) * 128], st.VT[0:C, par, h * 128:(h + 1) * 128])],
                        [st.KTTb[par], st.VTb[par]])
            s3 = S32[:, :].rearrange("p (h v) -> p h v", h=4)
            self.tt(S32[:, :], S32[:, :], KVp[:, :], ALU.add, [S32b, KVpb], [S32b])
            self.tt(s3, s3, st.EBL[:, :, c:c + 1].broadcast_to([128, 4, 128]), ALU.mult, [S32b, st.EBLb], [S32b])
            if st.isP:
                self.cp(S16[:, :], S32[:, :], [S32b], [S16b], en="act")
            else:
                self.dma("sp", self.d_hgS[l, c], S32[:, :], [S32b], [], self.misc(), is_output=True)

    def hg_out(self, st, l):
        cb = st.cellb
        H = range(4)
        oc = lambda h: 19 + h
        o2 = lambda h: self.cb16(st, 6 + h // 2, h % 2)
        o2b = lambda h: cb[6 + h // 2]
        rc = lambda h: 11 + h
        banks = [st.psSS, st.psVT[0], st.psSS, st.psVT[0]] if st.isP else [st.psSS] * 4
        for h in H:
            self.act(o2(h), self.cf(st, oc(h)), AF.Square, [cb[oc(h)]], [o2b(h)])
        for pair in (((0, 1), (2, 3)) if st.isP else ((0,), (1,), (2,), (3,))):
            for h in pair:
                ss, ssb = banks[h]
                self.mm(ss[:, 0:st.N], ssb, [(self.ones[:, :], o2(h))], [self.onesb, o2b(h)])
            for h in pair:
                ss, ssb = banks[h]
                self.act(self.cf(st, rc(h)), ss[:, 0:st.N], AF.Ln, [ssb, self.epsb], [cb[rc(h)]], scale=1.0 / 128,
                         bias=self.epsap)
        for h in H:
            self.act(self.cf(st, rc(h)), self.cf(st, rc(h)), AF.Exp, [cb[rc(h)]], [cb[rc(h)]], scale=-0.5)
        for h in H:
            self.stt(self.cf(st, rc(h)), self.cf(st, oc(h)), self.cvap(f"hgn{l}", h), self.cf(st, rc(h)), ALU.mult,
                     ALU.mult, [cb[oc(h)], self.cvb, cb[rc(h)]], [cb[rc(h)]])
        for h in H:
            self.tt(st.ycat[:, 4 + h, :], self.cf(st, rc(h)), self.cb16(st, 27 + h // 2, h % 2), ALU.mult,
                    [cb[rc(h)], cb[27 + h // 2]], [st.yb[4 + h]], en=(self.pen if h % 2 else "dve"))

    def final(self, st, dst):
        self.norm(st, st.xi, "fn", final=True)
        x = st.xres[st.xi]
        for k in range(NK):
            cell = 4 + k
            self.stt(self.cf(st, cell), x[:, k, :], self.cvap("fn", k), st.rstd[:, :], ALU.mult, ALU.mult,
                     [st.xb[st.xi][k], self.cvb, st.rstdb], [st.cellb[cell]])
            self.dma("sp", dst[k * 128:(k + 1) * 128, :], self.cf(st, cell), [st.cellb[cell]], [], self.ysem[k],
                     is_output=True)

    def build(self):
        nc = self.nc
        self.setup()
        self.eps_t = nc.alloc_sbuf_tensor("eps_t", [128, 1], F32)
        self.epsb = Buf()
        self.epsap = self.eps_t[:, 0:1]
        self.op("dve", lambda e: e.memset(self.eps_t[:, :], EPS), [], [self.epsb])
        self.one_t = nc.alloc_sbuf_tensor("one_t", [128, 1], F32)
        self.oneap = self.one_t[:, 0:1]
        self.op("dve", lambda e: e.memset(self.one_t[:, :], 1.0), [], [self.epsb])
        self.prologue()
        P, S = self.P, self.S
        self.load_x(0)
        for i in range(self.NT):
            P.xi = i % 2
            self.pen = "dve" if i == 0 else "pool"
            if i + 1 < self.NT:
                self.load_x(i + 1)
            streams = [P]
            if i == 0 and S is not None:
                S.xi = 0
                streams = [P, S]
            for l in range(NL):
                self.ffn(streams, l, 0)
                self.mixer(streams, l, first=(i == 0))
                self.ffn(streams, l, 1)
            self.final(P, self.d_yp[:, i * TP:(i + 1) * TP])
            if i == 0 and S is not None:
                self.final(S, self.d_ys[:, :])
                self.dma("sp", self.d_poolS[:, :], S.UH[:].rearrange("p l c s j -> p (l c s j)"), S.UHb, [],
                         self.misc(), is_output=True)
                self.dma("sp", self.d_convS[:, :], S.XH[:].rearrange("p l c s j -> p (l c s j)"), S.XHb, [],
                         self.misc(), is_output=True)
                self.dma("sp", self.d_lruS[:, :], S.HL[:].rearrange("p l c s -> p (l c s)"), S.HLb, [],
                         self.misc(), is_output=True)
        for l in range(NL):
            self.dma("sp", self.d_poolP[l], P.UH[:, l].rearrange("p c s j -> p (c s j)"), [P.UHb[l]], [],
                     self.misc(), is_output=True)
            self.dma("sp", self.d_convP[l], P.XH[:, l].rearrange("p c s j -> p (c s j)"), [P.XHb[l]], [],
                     self.misc(), is_output=True)
            self.dma("sp", self.d_lruP[l], P.HL[:, l].rearrange("p c s -> p (c s)"), [P.HLb[l]], [],
                     self.misc(), is_output=True)
            self.dma("sp", self.d_hgP[l], P.S32[l][0][:, :], [P.S32b[l][0]], [], self.misc(), is_output=True)
        self._wait("sp", self.out_events)
        return nc


def prep_inputs(inp, n_cores, NT, with_sample=True):
    T = NT * TP
    wpk = pack_weights(inp)
    cv, cm, bd = pack_consts(inp)
    maps = []
    for c in range(n_cores):
        m = {"wpk": wpk, "cv": cv, "cm": cm, "bd": bd,
             "xp": np.ascontiguousarray(inp["x_prompt"][c, :T].T)}
        if with_sample:
            sl = slice(NSEQ_S * c, NSEQ_S * c + NSEQ_S)
            m["xs"] = np.ascontiguousarray(inp["x_sample"][sl].reshape(NSEQ_S * LS, D).T)
            sp = inp["state_pool"][:, sl].reshape(NL, NSEQ_S, 15, 2, 128).transpose(4, 0, 3, 1, 2)
            m["spool"] = np.ascontiguousarray(sp).reshape(128, NL * 120)
            sc = inp["state_conv"][:, sl].reshape(NL, NSEQ_S, 3, 2, 128).transpose(4, 0, 3, 1, 2)
            m["sconv"] = np.ascontiguousarray(sc).reshape(128, NL * 24)
            slr = inp["state_lru"][:, sl].reshape(NL, NSEQ_S, 2, 128).transpose(3, 0, 2, 1)
            m["slru"] = np.ascontiguousarray(slr).reshape(128, NL * 8)
            sh = inp["state_hgrn"][:, sl].transpose(0, 1, 3, 2, 4)
            m["shg"] = np.ascontiguousarray(sh).reshape(NL, NSEQ_S, 128, 512)
        maps.append(m)
    return maps


def assemble(results, n_cores, NT, with_sample=True):
    T = NT * TP
    B = n_cores
    y_p = np.empty((B, T, D), np.float32)
    pool_p = np.empty((NL, B, 15, 256), np.float32)
    conv_p = np.empty((NL, B, 3, 256), np.float32)
    lru_p = np.empty((NL, B, 256), np.float32)
    hg_p = np.empty((NL, B, 4, 128, 128), np.float32)
    BS = NSEQ_S * n_cores
    y_s = np.empty((BS, LS, D), np.float32)
    pool_s = np.empty((NL, BS, 15, 256), np.float32)
    conv_s = np.empty((NL, BS, 3, 256), np.float32)
    lru_s = np.empty((NL, BS, 256), np.float32)
    hg_s = np.empty((NL, BS, 4, 128, 128), np.float32)
    for c in range(n_cores):
        r = results[c]
        y_p[c] = r["yp"].T
        pool_p[:, c] = r["poolP"].reshape(NL, 128, 2, 15).transpose(0, 3, 2, 1).reshape(NL, 15, 256)
        conv_p[:, c] = r["convP"].reshape(NL, 128, 2, 3).transpose(0, 3, 2, 1).reshape(NL, 3, 256)
        lru_p[:, c] = r["lruP"].reshape(NL, 128, 2).transpose(0, 2, 1).reshape(NL, 256)
        hg_p[:, c] = r["hgP"].reshape(NL, 128, 4, 128).transpose(0, 2, 1, 3)
        if with_sample:
            sl = slice(NSEQ_S * c, NSEQ_S * c + NSEQ_S)
            y_s[sl] = r["ys"].T.reshape(NSEQ_S, LS, D)
            ps = r["poolS"].reshape(128, NL, 2, NSEQ_S, 15).transpose(1, 3, 4, 2, 0)
            pool_s[:, sl] = ps.reshape(NL, NSEQ_S, 15, 256)
            cs = r["convS"].reshape(128, NL, 2, NSEQ_S, 3).transpose(1, 3, 4, 2, 0)
            conv_s[:, sl] = cs.reshape(NL, NSEQ_S, 3, 256)
            ls = r["lruS"].reshape(128, NL, 2, NSEQ_S).transpose(1, 3, 2, 0)
            lru_s[:, sl] = ls.reshape(NL, NSEQ_S, 256)
            hs = r["hgS"].reshape(NL, NSEQ_S, 128, 4, 128).transpose(0, 1, 3, 2, 4)
            hg_s[:, sl] = hs
    return (y_p, y_s, pool_p, conv_p, lru_p, hg_p, pool_s, conv_s, lru_s, hg_s)


def run(inp, n_cores=8, NT=16, with_sample=True):
    inp = {k: np.asarray(v) for k, v in inp.items()}
    nc = KB(NT, with_sample).build()
    maps = prep_inputs(inp, n_cores, NT, with_sample)
    res = run_bass_kernel_spmd(nc, maps, core_ids=list(range(n_cores)))
    return assemble(res.results, n_cores, NT, with_sample)


def kernel(**inputs):
    return run(inputs, 8, 16, True)
```

```python
import numpy as np
import concourse.bass as bass
import concourse.mybir as mybir
from concourse.bass_utils import run_bass_kernel_spmd

F32 = mybir.dt.float32
BF16 = mybir.dt.bfloat16
AF = mybir.ActivationFunctionType
ALU = mybir.AluOpType

D = 1024
DFF = 2816
NK = 8
NJ = 22
NL = 2
TP = 512
NSEQ_S = 4
LS = 16
EPS = 1e-6
NSLOT = 6
SLOT = 4096
NCELL = 29
GELU_C = 0.7978845608028654

PIECES = []
for _i in range(11):
    PIECES.append(("gu1", 4096))
for _i in range(8):
    PIECES.append(("d1", 2816))
PIECES += [("A", 2048), ("B", 4096), ("Q", 4096), ("G", 4096), ("F", 4096), ("V", 4096), ("O", 4096), ("O", 4096)]
for _i in range(11):
    PIECES.append(("gu2", 4096))
for _i in range(8):
    PIECES.append(("d2", 2816))
NPL = len(PIECES)
POFF = []
_o = 0
for _l in range(NL):
    for _n, _s in PIECES:
        POFF.append((_o, _s))
        _o += _s
WTOT = _o

CV = {}
_c = 0


def _cv(name, w):
    global _c
    CV[name] = (_c, w)
    _c += w


for _l in range(NL):
    for _nm, _w in [("n1", 8), ("n2", 8), ("n3", 8), ("psc", 2), ("cw", 8), ("cb", 2), ("ba", 2), ("bx", 2),
                    ("lam", 2), ("hgn", 4)]:
        _cv(f"{_nm}{_l}", _w)
_cv("fn", 8)
_cv("lg0", 4)
_cv("lg1", 4)
_cv("invw", 2)
_cv("corr", 32)
NCV = _c
CM_ID, CM_CA, CM_RP, CM_RS, CM_BM = 0, 128, 192, 704, 768
NCM = 896


def _chunkmat(W, c0, ncols):
    return np.ascontiguousarray(W[:, c0:c0 + ncols].reshape(8, 128, ncols).transpose(1, 0, 2)).reshape(128, 8 * ncols)


def _vec(v):
    n = v.shape[0] // 128
    return np.ascontiguousarray(v.reshape(n, 128).T)


def pack_weights(inp):
    out = np.empty((128, WTOT), np.float32)
    pi = 0
    for l in range(NL):
        def put(a):
            nonlocal pi
            off, sz = POFF[pi]
            assert a.shape == (128, sz), (a.shape, sz)
            out[:, off:off + sz] = a
            pi += 1

        for tag in ("ffn1", "mix", "ffn2"):
            if tag == "mix":
                W = inp["w_in"][l]
                put(np.concatenate([_chunkmat(W, c * 128, 128) for c in range(0, 2)], axis=1))
                put(np.concatenate([_chunkmat(W, c * 128, 128) for c in range(2, 6)], axis=1))
                put(np.concatenate([_chunkmat(W, c * 128, 128) for c in range(6, 10)], axis=1))
                put(np.concatenate([_chunkmat(W, c * 128, 128) for c in range(18, 22)], axis=1))
                put(np.concatenate([_chunkmat(W, c * 128, 128) for c in range(10, 14)], axis=1))
                put(_chunkmat(W, 14 * 128, 512))
                Wo = inp["w_out"][l]
                for p in range(2):
                    put(np.concatenate([_chunkmat(Wo, c * 128, 128) for c in range(4 * p, 4 * p + 4)], axis=1))
            else:
                Wg, Wu, Wd = inp[f"{tag}_w_gate"][l], inp[f"{tag}_w_up"][l], inp[f"{tag}_w_down"][l]
                for jp in range(11):
                    parts = []
                    for jj in range(2):
                        j = 2 * jp + jj
                        parts.append(_chunkmat(Wg, j * 128, 128))
                        parts.append(_chunkmat(Wu, j * 128, 128))
                    put(np.concatenate(parts, axis=1))
                for c in range(8):
                    a = Wd[:, c * 128:(c + 1) * 128].reshape(22, 128, 128).transpose(1, 0, 2).reshape(128, 2816)
                    put(np.ascontiguousarray(a))
    assert pi == NL * NPL
    return out


def pack_consts(inp):
    cv = np.zeros((128, NCV), np.float32)

    def put(name, a):
        off, w = CV[name]
        cv[:, off:off + w] = a.reshape(128, w)

    for l in range(NL):
        put(f"n1{l}", _vec(inp["ffn1_norm"][l]))
        put(f"n2{l}", _vec(inp["mix_norm"][l]))
        put(f"n3{l}", _vec(inp["ffn2_norm"][l]))
        put(f"psc{l}", _vec(inp["pool_scale"][l]))
        put(f"cw{l}", np.ascontiguousarray(inp["conv_w"][l].reshape(4, 2, 128).transpose(2, 1, 0)))
        put(f"cb{l}", _vec(inp["conv_b"][l]))
        put(f"ba{l}", _vec(inp["lru_b_a"][l]))
        put(f"bx{l}", _vec(inp["lru_b_x"][l]))
        put(f"lam{l}", _vec(inp["lru_lambda"][l]))
        put(f"hgn{l}", _vec(inp["hgrn_norm"][l]))
    put("fn", _vec(inp["final_norm"]))
    put("lg0", _vec(inp["hgrn_lb_logits"][0]))
    put("lg1", _vec(inp["hgrn_lb_logits"][1]))
    wins = np.array([2, 4, 8, 16], np.float32)
    invw = np.zeros((128, 2), np.float32)
    corr = np.zeros((128, 2, 16), np.float32)
    for ch in range(2):
        for half in range(2):
            w = wins[2 * ch + half]
            invw[64 * half:64 * half + 64, ch] = 1.0 / w
            for pos in range(16):
                corr[64 * half:64 * half + 64, ch, pos] = w / min(w, pos + 1)
    put("invw", invw)
    put("corr", corr)
    cm = np.zeros((128, NCM), np.float32)
    cm[:, CM_ID:CM_ID + 128] = np.eye(128, dtype=np.float32)
    cm[:64, CM_CA:CM_CA + 64] = np.triu(np.ones((64, 64), np.float32))
    rp = np.ones(512, np.float32)
    rp[0::64] = 0.0
    rs = np.ones(64, np.float32)
    rs[0::16] = 0.0
    cm[:, CM_RP:CM_RP + 512] = rp[None]
    cm[:, CM_RS:CM_RS + 64] = rs[None]
    cm[0:64, CM_BM:CM_BM + 64] = np.triu(np.ones((64, 64), np.float32))
    cm[64:128, CM_BM + 64:CM_BM + 128] = np.triu(np.ones((64, 64), np.float32))
    bd = np.zeros((128, NL, 6, 128), np.float32)
    for l in range(NL):
        for mi, nm in enumerate(["pool_w", "lru_w_a", "lru_w_x"]):
            w = inp[nm][l]
            for ch in range(2):
                for half in range(2):
                    bd[64 * half:64 * half + 64, l, 2 * mi + ch, 64 * half:64 * half + 64] = w[2 * ch + half]
    return cv, cm, bd.reshape(128, NL * 6 * 128)


class Buf:
    __slots__ = ("w", "r")

    def __init__(self):
        self.w = None
        self.r = {}


class DSem:
    def __init__(self, sem):
        self.sem = sem
        self.val = 0


class Stream:
    pass


class KB:
    def __init__(self, NT, with_sample=True):
        self.NT = NT
        self.with_sample = with_sample
        nc = bass.Bass("TRN2", target_bir_lowering=False)
        self.nc = nc
        self.eng = dict(pe=nc.tensor, act=nc.scalar, dve=nc.vector, pool=nc.gpsimd, sp=nc.sync)
        self.esem = {k: nc.alloc_semaphore(name=f"es_{k}") for k in self.eng}
        self.ecnt = {k: 0 for k in self.eng}
        self.waited = {k: {} for k in self.eng}
        self.out_events = []
        self._ds_n = 0

    def dsem(self):
        self._ds_n += 1
        return DSem(self.nc.alloc_semaphore(name=f"ds{self._ds_n}"))

    def _wait(self, en, deps):
        w = self.waited[en]
        best = {}
        for sem, val in deps:
            k = sem.num
            if val > w.get(k, 0) and val > best.get(k, (None, 0))[1]:
                best[k] = (sem, val)
        for k, (sem, val) in best.items():
            self.eng[en].wait_ge(sem, val)
            w[k] = val

    def _deps(self, en, reads, writes):
        own = self.esem[en].num
        deps = []
        for b in reads:
            if b.w is not None:
                deps.append(b.w)
        for b in writes:
            if b.w is not None:
                deps.append(b.w)
            for ev in b.r.values():
                deps.append(ev)
        if en == "pe":
            deps = [d for d in deps if d[0].num != own]
        return deps

    def op(self, en, fn, reads=(), writes=()):
        self._wait(en, self._deps(en, reads, writes))
        ins = fn(self.eng[en])
        self.ecnt[en] += 1
        ev = (self.esem[en], self.ecnt[en])
        ins.then_inc(ev[0], 1)
        own = ev[0].num
        for b in writes:
            b.w = ev
            b.r = {}
        for b in reads:
            b.r[own] = ev
        return ev

    def mm(self, out_ap, outbuf, pairs, reads, pair_reads=None):
        self._wait("pe", self._deps("pe", reads, [outbuf]))
        n = len(pairs)
        ins = None
        for i, (lhsT, rhs) in enumerate(pairs):
            if pair_reads is not None:
                self._wait("pe", self._deps("pe", pair_reads[i], []))
            ins = self.nc.tensor.matmul(out_ap, lhsT=lhsT, rhs=rhs, start=(i == 0), stop=(i == n - 1))
        if pair_reads is not None:
            reads = list(reads) + [b for pr in pair_reads for b in pr]
        self.ecnt["pe"] += 1
        ev = (self.esem["pe"], self.ecnt["pe"])
        ins.then_inc(ev[0], 1)
        outbuf.w = ev
        outbuf.r = {}
        for b in reads:
            b.r[ev[0].num] = ev
        return ev

    def mm_kouter(self, st, w, wb, groups, rhs_t=None, rhs_b=None):
        self._wait("pe", self._deps("pe", [wb], [g[1] for g in groups]))
        evs = []
        rhs_t = st.h if rhs_t is None else rhs_t
        rhs_b = st.hb if rhs_b is None else rhs_b
        for k in range(NK):
            self._wait("pe", self._deps("pe", [rhs_b[k]], []))
            for out_ap, ob, c0 in groups:
                ins = self.nc.tensor.matmul(out_ap, lhsT=w[:, c0 + k * 128:c0 + (k + 1) * 128], rhs=rhs_t[:, k, :],
                                            start=(k == 0), stop=(k == NK - 1))
                if k == NK - 1:
                    self.ecnt["pe"] += 1
                    ev = (self.esem["pe"], self.ecnt["pe"])
                    ins.then_inc(ev[0], 1)
                    ob.w = ev
                    ob.r = {}
                    evs.append(ev)
        for b in [wb] + list(rhs_b):
            b.r[evs[-1][0].num] = evs[-1]

    def dma(self, q, out, in_, reads, writes, ds, is_output=False):
        deps = self._deps(q, reads, writes)
        if ds.val > 0:
            deps.append((ds.sem, ds.val))
        self._wait(q, deps)
        ins = self.eng[q].dma_start(out=out, in_=in_)
        ds.val += 16
        ev = (ds.sem, ds.val)
        ins.then_inc(ds.sem, 16)
        for b in writes:
            b.w = ev
            b.r = {}
        for b in reads:
            b.r[ds.sem.num] = ev
        if is_output:
            self.out_events.append(ev)
        return ev

    def act(self, out, in_, func, reads, writes, scale=None, bias=None):
        kw = {}
        if scale is not None:
            kw["scale"] = scale
        if bias is not None:
            kw["bias"] = bias
        return self.op("act", lambda e: e.activation(out=out, in_=in_, func=func, **kw), reads, writes)

    def tt(self, out, in0, in1, op, reads, writes, en="dve"):
        return self.op(en, lambda e: e.tensor_tensor(out=out, in0=in0, in1=in1, op=op), reads, writes)

    def stt(self, out, in0, scalar, in1, op0, op1, reads, writes):
        return self.op("dve", lambda e: e.scalar_tensor_tensor(out=out, in0=in0, scalar=scalar, in1=in1,
                                                               op0=op0, op1=op1), reads, writes)

    def ts(self, out, in0, s1, s2, op0, op1, reads, writes, en="dve"):
        if op1 is None:
            return self.op(en, lambda e: e.tensor_scalar(out=out, in0=in0, scalar1=s1, scalar2=None, op0=op0),
                           reads, writes)
        return self.op(en, lambda e: e.tensor_scalar(out=out, in0=in0, scalar1=s1, scalar2=s2, op0=op0, op1=op1),
                       reads, writes)

    def cp(self, out, in_, reads, writes, en="dve"):
        if en == "act":
            return self.op("act", lambda e: e.copy(out=out, in_=in_), reads, writes)
        return self.op(en, lambda e: e.tensor_copy(out=out, in_=in_), reads, writes)

    def cvap(self, name, i=0, n=1):
        off, w = CV[name]
        return self.cv[:, off + i:off + i + n]

    def setup(self):
        nc = self.nc
        NT = self.NT
        T = NT * TP
        self.T = T
        dt = nc.dram_tensor
        self.d_xp = dt("xp", [D, T], F32, kind="ExternalInput").ap()
        self.d_wpk = dt("wpk", [128, WTOT], F32, kind="ExternalInput").ap()
        self.d_cv = dt("cv", [128, NCV], F32, kind="ExternalInput").ap()
        self.d_cm = dt("cm", [128, NCM], F32, kind="ExternalInput").ap()
        self.d_bd = dt("bd", [128, NL * 768], F32, kind="ExternalInput").ap()
        self.d_wbf = dt("wbf", [128, WTOT], BF16, kind="Internal").ap()
        self.d_yp = dt("yp", [D, T], F32, kind="ExternalOutput").ap()
        self.d_poolP = dt("poolP", [NL, 128, 30], F32, kind="ExternalOutput").ap()
        self.d_convP = dt("convP", [NL, 128, 6], F32, kind="ExternalOutput").ap()
        self.d_lruP = dt("lruP", [NL, 128, 2], F32, kind="ExternalOutput").ap()
        self.d_hgP = dt("hgP", [NL, 128, 512], F32, kind="ExternalOutput").ap()
        if self.with_sample:
            self.d_xs = dt("xs", [D, 64], F32, kind="ExternalInput").ap()
            self.d_spool = dt("spool", [128, NL * 120], F32, kind="ExternalInput").ap()
            self.d_sconv = dt("sconv", [128, NL * 24], F32, kind="ExternalInput").ap()
            self.d_slru = dt("slru", [128, NL * 8], F32, kind="ExternalInput").ap()
            self.d_shg = dt("shg", [NL, NSEQ_S, 128, 512], F32, kind="ExternalInput").ap()
            self.d_ys = dt("ys", [D, 64], F32, kind="ExternalOutput").ap()
            self.d_poolS = dt("poolS", [128, NL * 120], F32, kind="ExternalOutput").ap()
            self.d_convS = dt("convS", [128, NL * 24], F32, kind="ExternalOutput").ap()
            self.d_lruS = dt("lruS", [128, NL * 8], F32, kind="ExternalOutput").ap()
            self.d_hgS = dt("hgS", [NL, NSEQ_S, 128, 512], F32, kind="ExternalOutput").ap()

        sb = nc.alloc_sbuf_tensor
        self.cv = sb("cv_sb", [128, NCV], F32)
        self.cm = sb("cm_sb", [128, NCM], F32)
        self.cvb = Buf()
        self.cmb = Buf()
        self.bdb = sb("bdb", [128, NL * 768], BF16)
        self.bdbb = Buf()
        self.ones = sb("ones", [128, 128], BF16)
        self.onesb = Buf()
        self.ident = sb("ident", [128, 128], BF16)
        self.identb = Buf()
        self.dv = sb("dv", [128, 40], F32)
        self.dvb = Buf()
        self.wring = sb("wring", [128, NSLOT, SLOT], BF16)
        self.wbuf = [Buf() for _ in range(NSLOT)]
        self.wsem = [self.dsem() for _ in range(NSLOT)]
        self.wbf_buf = [Buf() for _ in range(NL * NPL)]
        self.cast_sem = [self.dsem() for _ in range(8)]
        self.misc_sem = [self.dsem() for _ in range(4)]
        self._misc_i = 0
        self.ysem = [self.dsem() for _ in range(8)]
        self.xsem = [self.dsem() for _ in range(2)]
        self.ps = [nc.alloc_psum_tensor(f"ps{i}", [128, 512], F32) for i in range(7)]
        self.ps7 = nc.alloc_psum_tensor("ps7", [128, 1024], BF16)
        self.pb = [Buf() for _ in range(8)]
        self.psS_b = [Buf() for _ in range(8)]
        self.P = self.mk_stream("P", 1, TP, 64, True)
        self.S = self.mk_stream("S", NSEQ_S, LS, 16, False) if self.with_sample else None

    def misc(self):
        self._misc_i += 1
        return self.misc_sem[self._misc_i % 4]

    def mk_stream(self, name, nseq, Ls, C, isP):
        nc = self.nc
        sb = nc.alloc_sbuf_tensor
        st = Stream()
        st.name, st.nseq, st.Ls, st.C, st.isP = name, nseq, Ls, C, isP
        N = nseq * Ls
        st.N = N
        st.nch = N // C
        st.CW = ((max(N + 16, nseq * (Ls + 15)) + 7) // 8) * 8
        nx = 2 if isP else 1
        st.xres = [sb(f"xres{name}{i}", [128, NK, N], F32) for i in range(nx)]
        st.xb = [[Buf() for _ in range(NK)] for _ in range(nx)]
        st.h = sb(f"h{name}", [128, NK, N], BF16)
        st.hb = [Buf() for _ in range(NK)]
        st.ycat = sb(f"ycat{name}", [128, NK, N], BF16)
        st.yb = [Buf() for _ in range(NK)]
        st.rstd = sb(f"rstd{name}", [128, N], F32)
        st.rstdb = Buf()
        st.arena = sb(f"arena{name}", [128, NCELL, st.CW], F32)
        st.cellb = [Buf() for _ in range(NCELL)]
        st.UH = sb(f"UH{name}", [128, NL, 2, nseq, 15], F32)
        st.UHb = [Buf() for _ in range(NL)]
        st.XH = sb(f"XH{name}", [128, NL, 2, nseq, 3], F32)
        st.XHb = [Buf() for _ in range(NL)]
        st.HL = sb(f"HL{name}", [128, NL, 2, nseq], F32)
        st.HLb = [Buf() for _ in range(NL)]
        if isP:
            st.S32 = [[sb(f"S32{name}{l}", [128, 512], F32)] for l in range(NL)]
            st.S32b = [[Buf()] for l in range(NL)]
            st.S16 = [[sb(f"S16{name}{l}", [128, 512], BF16)] for l in range(NL)]
            st.S16b = [[Buf()] for l in range(NL)]
        else:
            t32 = [sb(f"S32{name}{s}", [128, 512], F32) for s in range(2)] * 2
            t16 = [sb(f"S16{name}{s}", [128, 512], BF16) for s in range(2)] * 2
            b32 = [Buf() for _ in range(2)] * 2
            b16 = [Buf() for _ in range(2)] * 2
            st.S32 = [t32 for l in range(NL)]
            st.S32b = [b32 for l in range(NL)]
            st.S16 = [t16 for l in range(NL)]
            st.S16b = [b16 for l in range(NL)]
        st.EBL = sb(f"EBL{name}", [128, 4, st.nch], F32)
        st.EBLb = Buf()
        if isP:
            st.VT = sb(f"VT{name}", [128, 4, 512], BF16)
            st.VTb = [Buf() for _ in range(4)]
            st.KTT = sb(f"KTT{name}", [128, 2, 512], BF16)
            st.KTTb = [Buf(), Buf()]
            st.ATT = sb(f"ATT{name}", [128, 2, 512], BF16)
            st.ATTb = [Buf(), Buf()]
        else:
            st.VT, st.VTb, st.KTT, st.KTTb, st.ATT, st.ATTb = (self.P.VT, self.P.VTb, self.P.KTT, self.P.KTTb,
                                                                 self.P.ATT, self.P.ATTb)
        st.O2 = sb(f"O2{name}", [128, N], BF16)
        st.O2b = Buf()
        if isP:
            bank = lambda i: (self.ps[i][:, 0:N], self.pb[i])
            st.psG = [bank(0), bank(1)]
            st.psU = [bank(2), bank(3)]
            st.psZ = [bank(0), bank(1), bank(2), bank(3)]
            st.psSS = bank(4)
            st.psATT = (self.ps[1], self.pb[1])
            st.psOC = [(self.ps[2][:, 0:256], self.pb[2]), (self.ps[2][:, 256:512], self.pb[2])]
        else:
            sub = lambda i: (self.ps[5][:, 64 * i:64 * i + 64], self.psS_b[0])
            st.psG = [sub(0), sub(1)]
            st.psU = [sub(2), sub(3)]
            st.psZ = [sub(0), sub(1), sub(2), sub(3)]
            st.psSS = sub(4)
            st.psATT = (self.ps[5][:, 320:384], self.psS_b[0])
            st.psOC = [sub(6), sub(7)]
        st.psVT = [(self.ps[6], self.pb[6]), (self.ps[0], self.pb[0])]
        st.psKTT = (self.ps7, self.pb[7])
        st.psKV = (self.ps[3], self.pb[3])
        return st

    def cf(self, st, i, a=0, n=None):
        n = st.N if n is None else n
        return st.arena[:, i, a:a + n]

    def cb16(self, st, i, half):
        return st.arena[:, i, :].bitcast(BF16)[:, half * st.CW: half * st.CW + st.N]

    def seg(self, st, ap):
        return ap.rearrange("p (s l) -> p s l", s=st.nseq)

    def start_weights(self):
        for p in range(NL * NPL):
            off, sz = POFF[p]
            self.dma("pool", self.d_wbf[:, off:off + sz], self.d_wpk[:, off:off + sz], [], [self.wbf_buf[p]],
                     self.cast_sem[p % 8])
        self.pc = 0
        self.pl = 0
        self.total_pieces = self.NT * NL * NPL
        for _ in range(NSLOT):
            self._load_next()

    def _load_next(self):
        g = self.pl
        if g >= self.total_pieces:
            return
        self.pl += 1
        p = g % (NL * NPL)
        off, sz = POFF[p]
        s = g % NSLOT
        self.dma("sp", self.wring[:, s, 0:sz], self.d_wbf[:, off:off + sz], [self.wbf_buf[p]], [self.wbuf[s]],
                 self.wsem[s])

    def take(self, expect):
        g = self.pc
        p = g % (NL * NPL)
        assert PIECES[p % NPL][0] == expect, (PIECES[p % NPL][0], expect)
        s = g % NSLOT
        return self.wring[:, s, :], self.wbuf[s]

    def release(self):
        self.pc += 1
        self._load_next()

    def prologue(self):
        nc = self.nc
        self.dma("sp", self.cv[:, :], self.d_cv[:, :], [], [self.cvb], self.misc())
        self.dma("sp", self.cm[:, :], self.d_cm[:, :], [], [self.cmb], self.misc())
        bd32 = self.P.arena[:, 0:3, 0:512]
        self.dma("sp", bd32, self.d_bd.rearrange("p (c w) -> p c w", c=3), [], self.P.cellb[0:3], self.misc())
        self.start_weights()
        self.op("dve", lambda e: e.memset(self.ones[:, :], 1.0), [], [self.onesb])
        self.cp(self.ident[:, :], self.cm[:, CM_ID:CM_ID + 128], [self.cmb], [self.identb])
        self.cp(self.bdb[:, :].rearrange("p (c w) -> p c w", c=3), bd32, self.P.cellb[0:3], [self.bdbb])
        dv = self.dv
        R, W = [self.cvb, self.dvb], [self.dvb]
        self.op("dve", lambda e: e.memset(dv[:, :], 0.0), [], W)
        self.tt(dv[:, 4:8], self.cvap("lg1", 0, 4), self.cvap("lg0", 0, 4), ALU.subtract, R, W)
        self.act(dv[:, 4:8], dv[:, 4:8], AF.Sigmoid, R, W)
        self.ts(dv[:, 8:16], dv[:, 0:8], -1.0, 1.0, ALU.mult, ALU.add, R, W)
        for l in range(NL):
            o = 16 + 2 * l
            self.act(dv[:, o:o + 2], self.cvap(f"lam{l}", 0, 2), AF.Exp, R, W, scale=-1.0)
            self.ts(dv[:, o:o + 2], dv[:, o:o + 2], 1.0, None, ALU.add, None, R, W)
            self.act(dv[:, o:o + 2], dv[:, o:o + 2], AF.Ln, R, W)
            self.ts(dv[:, o:o + 2], dv[:, o:o + 2], -8.0, None, ALU.mult, None, R, W)
        self.ts(dv[:, 20:24], dv[:, 16:20], 2.0, None, ALU.mult, None, R, W)
        self.ts(dv[:, 24:32], dv[:, 8:16], -1.0, None, ALU.mult, None, R, W)
        self.ts(dv[:, 32:40], dv[:, 0:8], 1e-19, None, ALU.add, None, R, W)
        P = self.P
        for l in range(NL):
            self.op("dve", lambda e: e.memset(P.UH[:, l].rearrange("p c s j -> p (c s j)"), 0.0), [], [P.UHb[l]])
            self.op("dve", lambda e: e.memset(P.XH[:, l].rearrange("p c s j -> p (c s j)"), 0.0), [], [P.XHb[l]])
            self.op("dve", lambda e: e.memset(P.HL[:, l].rearrange("p c s -> p (c s)"), 0.0), [], [P.HLb[l]])
            self.op("dve", lambda e: e.memset(P.S32[l][0][:, :], 0.0), [], [P.S32b[l][0]])
            self.op("dve", lambda e: e.memset(P.S16[l][0][:, :], 0.0), [], [P.S16b[l][0]])
        if self.S is not None:
            S = self.S
            self.dma("sp", S.xres[0][:, :, :], self.d_xs.rearrange("(k p) t -> p k t", p=128), [], S.xb[0],
                     self.misc())
            self.dma("sp", S.UH[:].rearrange("p l c s j -> p (l c s j)"), self.d_spool[:, :], [], S.UHb, self.misc())
            self.dma("sp", S.XH[:].rearrange("p l c s j -> p (l c s j)"), self.d_sconv[:, :], [], S.XHb, self.misc())
            self.dma("sp", S.HL[:].rearrange("p l c s -> p (l c s)"), self.d_slru[:, :], [], S.HLb, self.misc())

    def load_x(self, i):
        P = self.P
        par = i % 2
        src = self.d_xp.rearrange("(k p) t -> p k t", p=128)[:, :, i * TP:(i + 1) * TP]
        self.dma("sp", P.xres[par][:, :, :], src, [], P.xb[par], self.xsem[par])

    def norm(self, st, xi, gname, final=False, presilu=False):
        x = st.xres[xi]
        xb = st.xb[xi]
        N = st.N
        for k in range(NK):
            self.act(self.cb16(st, 13 + k // 2, k % 2), x[:, k, :], AF.Square, [xb[k]], [st.cellb[13 + k // 2]])
        ss, ssb = st.psSS
        self.mm(ss, ssb, [(self.ones[:, :], self.cb16(st, 13 + k // 2, k % 2)) for k in range(NK)],
                [self.onesb], [[st.cellb[13 + k // 2]] for k in range(NK)])
        self.act(st.rstd[:, :], ss, AF.Ln, [ssb, self.epsb], [st.rstdb], scale=1.0 / D, bias=self.epsap)
        self.act(st.rstd[:, :], st.rstd[:, :], AF.Exp, [st.rstdb], [st.rstdb], scale=-0.5)
        if not final:
            for k in range(NK):
                self.stt(st.h[:, k, :], x[:, k, :], self.cvap(gname, k), st.rstd[:, :], ALU.mult, ALU.mult,
                         [xb[k], self.cvb, st.rstdb], [st.hb[k]])

    def ffn(self, streams, l, which):
        tag = "gu1" if which == 0 else "gu2"
        dtag = "d1" if which == 0 else "d2"
        gname = f"n1{l}" if which == 0 else f"n3{l}"
        for st in streams:
            self.norm(st, st.xi, gname, presilu=True)
        for jp in range(11):
            w, wb = self.take(tag)
            if jp == 0:
                for st in streams:
                    if not st.isP:
                        continue
                    groups = []
                    for jj in range(2):
                        groups.append((st.psG[jj][0], st.psG[jj][1], (2 * jj) * 1024))
                        groups.append((st.psU[jj][0], st.psU[jj][1], (2 * jj + 1) * 1024))
                    self.mm_kouter(st, w, wb, groups)
            for jj in range(2):
                j = 2 * jp + jj
                for st in streams:
                    G, Gb = st.psG[j % 2]
                    U, Ub = st.psU[j % 2]
                    g0 = (2 * jj) * 1024
                    u0 = (2 * jj + 1) * 1024
                    if jp > 0 or not st.isP:
                      self.mm(G, Gb, [(w[:, g0 + k * 128:g0 + (k + 1) * 128], st.h[:, k, :]) for k in range(NK)],
                            [wb], [[st.hb[k]] for k in range(NK)])
                      self.mm(U, Ub, [(w[:, u0 + k * 128:u0 + (k + 1) * 128], st.h[:, k, :]) for k in range(NK)],
                            [wb] + st.hb)
                    sgc = 11 + (j % 2)
                    self.act(self.cf(st, sgc), G, AF.Silu, [Gb], [st.cellb[sgc]])
                    self.tt(self.cb16(st, j // 2, j % 2), self.cf(st, sgc), U, ALU.mult, [st.cellb[sgc], Ub],
                            [st.cellb[j // 2]])
            self.release()
        for c in range(NK):
            w, wb = self.take(dtag)
            for st in streams:
                Dp, Db = st.psG[c % 2]
                self.mm(Dp, Db, [(w[:, j * 128:(j + 1) * 128], self.cb16(st, j // 2, j % 2)) for j in range(NJ)],
                        [wb], [[st.cellb[j // 2]] for j in range(NJ)])
                x = st.xres[st.xi]
                self.stt(x[:, c, :], Dp, 0.5, x[:, c, :], ALU.mult, ALU.add, [Db, st.xb[st.xi][c]],
                         [st.xb[st.xi][c]])
            self.release()

    def zmm(self, st, w, wb, cl, zi):
        Z, Zb = st.psZ[zi % 4]
        self.mm(Z, Zb, [(w[:, (cl * 8 + k) * 128:(cl * 8 + k + 1) * 128], st.h[:, k, :]) for k in range(NK)],
                [wb], [[st.hb[k]] for k in range(NK)])
        return Z, Zb

    def mixer(self, streams, l, first):
        for st in streams:
            self.norm(st, st.xi, f"n2{l}")
        pe = self.pen
        w, wb = self.take("A")
        for st in streams:
            cb = st.cellb
            for ch in range(2):
                Z, Zb = self.zmm(st, w, wb, ch, ch)
                self.cp(self.pv3(st, ch, 0, 15), st.UH[:, l, ch], [st.UHb[l]], [cb[ch]], en=pe)
                self.cp(self.pv3(st, ch, 15, st.Ls), self.seg(st, Z), [Zb], [cb[ch]], en="act")
        self.release()
        w, wb = self.take("B")
        for st in streams:
            cb = st.cellb
            for ch in range(2):
                Z, Zb = self.zmm(st, w, wb, ch, ch)
                self.cp(self.lv3(st, 7 + ch, 0, 3), st.XH[:, l, ch], [st.XHb[l]], [cb[7 + ch]], en=pe)
                self.cp(self.lv3(st, 7 + ch, 3, st.Ls), self.seg(st, Z), [Zb], [cb[7 + ch]], en="act")
            for ch in range(2):
                Zg, Zgb = self.zmm(st, w, wb, 2 + ch, 2 + ch)
                self.act(self.cf(st, 9 + ch), Zg, AF.Gelu_apprx_tanh, [Zgb], [cb[9 + ch]])
        self.release()
        w, wb = self.take("Q")
        for st in streams:
            for h in range(4):
                Z, Zb = self.zmm(st, w, wb, h, h)
                self.act(self.cf(st, 23 + h), Z, AF.Silu, [Zb], [st.cellb[23 + h]])
        self.release()
        w, wb = self.take("G")
        for st in streams:
            for h in range(4):
                Z, Zb = self.zmm(st, w, wb, h, h)
                self.act(self.cb16(st, 27 + h // 2, h % 2), Z, AF.Silu, [Zb], [st.cellb[27 + h // 2]])
        self.release()
        w, wb = self.take("F")
        for st in streams:
            for h in range(4):
                Z, Zb = self.zmm(st, w, wb, h, h)
                self.act(self.cf(st, 19 + h), Z, AF.Sigmoid, [Zb], [st.cellb[19 + h]])
        self.release()
        w, wb = self.take("V")
        self.hg_vtok_P(self.P, w, wb)
        for st in streams:
            self.lru_gates(st, l)
        for st in streams:
            self.pool_chain(st, l, first and st.isP)
        for st in streams:
            self.lru_chain(st, l, first and st.isP)
        for st in streams:
            self.hg_prep(st, l)
        for st in streams:
            if st.isP:
                self.hg_chunks_P(st, l)
            else:
                self.hg_chunks(st, l, w, wb)
        self.release()
        for st in streams:
            self.hg_out(st, l)
        for p in range(2):
            w, wb = self.take("O")
            if p == 0:
                P_ = self.P
                self.mm_kouter(P_, w, wb, [(P_.psZ[cl][0], P_.psZ[cl][1], cl * 1024) for cl in range(4)],
                               rhs_t=P_.ycat, rhs_b=P_.yb)
            for cl in range(4):
                c = 4 * p + cl
                for st in streams:
                    Z, Zb = st.psZ[c % 4]
                    if p > 0 or not st.isP:
                      self.mm(Z, Zb, [(w[:, (cl * 8 + k) * 128:(cl * 8 + k + 1) * 128], st.ycat[:, k, :])
                                    for k in range(NK)], [wb], [[st.yb[k]] for k in range(NK)])
                    x = st.xres[st.xi]
                    self.tt(x[:, c, :], Z, x[:, c, :], ALU.add, [Zb, st.xb[st.xi][c]], [st.xb[st.xi][c]])
            self.release()

    def pv3(self, st, cell, a, n):
        LW = 15 + st.Ls
        return st.arena[:, cell, 0:st.nseq * LW].rearrange("p (s l) -> p s l", s=st.nseq)[:, :, a:a + n]

    def lv3(self, st, cell, a, n):
        LW = 3 + st.Ls
        return st.arena[:, cell, 0:st.nseq * LW].rearrange("p (s l) -> p s l", s=st.nseq)[:, :, a:a + n]

    def pool_chain(self, st, l, first):
        Ls = st.Ls
        HP = 15
        LW = HP + Ls
        cb = st.cellb
        pe = self.pen
        v3 = lambda cell, a, n: self.pv3(st, cell, a, n)
        for ch in range(2):
            ub = ch
            self.tt(v3(2, 1, LW - 1), v3(ub, 1, LW - 1), v3(ub, 0, LW - 1), ALU.add, [cb[ub]], [cb[2]], en=pe)
            self.tt(v3(3, 3, LW - 3), v3(2, 3, LW - 3), v3(2, 1, LW - 3), ALU.add, [cb[2]], [cb[3]], en=pe)
            if ch == 1:
                self.tt(v3(4, 7, LW - 7), v3(3, 7, LW - 7), v3(3, 3, LW - 7), ALU.add, [cb[3]], [cb[4]], en=pe)
                self.tt(v3(5, 15, LW - 15), v3(4, 15, LW - 15), v3(4, 7, LW - 15), ALU.add, [cb[4]], [cb[5]], en=pe)
                lo, hi = 4, 5
            else:
                lo, hi = 2, 3
            if first:
                corr = self.cv[:, CV["corr"][0] + 16 * ch: CV["corr"][0] + 16 * ch + 16]
                for cell, p0 in ((lo, 0), (hi, 64)):
                    a = st.arena[p0:p0 + 64, cell, HP:HP + 16]
                    self.tt(a, a, corr[p0:p0 + 64, :], ALU.mult, [cb[cell], self.cvb], [cb[cell]], en=pe)
            pl = self.seg(st, self.cb16(st, 6, ch))
            for cell, p0 in ((lo, 0), (hi, 64)):
                self.stt(pl[p0:p0 + 64], v3(cell, HP, Ls)[p0:p0 + 64], self.cvap("invw", ch)[p0:p0 + 64],
                         v3(ub, HP, Ls)[p0:p0 + 64], ALU.mult, ALU.subtract, [cb[cell], cb[ub], self.cvb], [cb[6]])
            self.cp(st.UH[:, l, ch], v3(ub, Ls, HP), [cb[ub]], [st.UHb[l]], en=pe)
            Y, Yb = st.psZ[2 + ch]
            self.mm(Y, Yb, [(self.bdb[:, (l * 6 + ch) * 128:(l * 6 + ch + 1) * 128], self.cb16(st, 6, ch))],
                    [self.bdbb, cb[6]])
            self.act(st.ycat[:, ch, :], Y, AF.Identity, [Yb, self.cvb], [st.yb[ch]], scale=self.cvap(f"psc{l}", ch))

    def lru_gates(self, st, l):
        Ls = st.Ls
        cb = st.cellb
        c3 = lambda cell: self.seg(st, self.cf(st, cell))
        for ch in range(2):
            xbc = 7 + ch
            cvc = 11 + ch
            cw = CV[f"cw{l}"][0] + 4 * ch
            self.ts(c3(cvc), self.lv3(st, xbc, 0, Ls), self.cv[:, cw:cw + 1], self.cvap(f"cb{l}", ch), ALU.mult,
                    ALU.add, [cb[xbc], self.cvb], [cb[cvc]])
            for k in range(1, 4):
                self.stt(c3(cvc), self.lv3(st, xbc, k, Ls), self.cv[:, cw + k:cw + k + 1], c3(cvc), ALU.mult, ALU.add,
                         [cb[xbc], self.cvb, cb[cvc]], [cb[cvc]])
            self.cp(st.XH[:, l, ch], self.lv3(st, xbc, Ls, 3), [cb[xbc]], [st.XHb[l]], en=self.pen)
            cvb16 = self.cb16(st, 17, ch)
            self.cp(cvb16, self.cf(st, cvc), [cb[cvc]], [cb[17]], en="act")
            Rp, Rb = st.psZ[ch]
            self.mm(Rp, Rb, [(self.bdb[:, (l * 6 + 2 + ch) * 128:(l * 6 + 3 + ch) * 128], cvb16)], [self.bdbb, cb[17]])
            self.act(self.cf(st, 13 + ch), Rp, AF.Sigmoid, [Rb, self.cvb], [cb[13 + ch]], bias=self.cvap(f"ba{l}", ch))
            Ip, Ib = st.psZ[2 + ch]
            self.mm(Ip, Ib, [(self.bdb[:, (l * 6 + 4 + ch) * 128:(l * 6 + 5 + ch) * 128], cvb16)], [self.bdbb, cb[17]])
            self.act(self.cf(st, 15 + ch), Ip, AF.Sigmoid, [Ib, self.cvb], [cb[15 + ch]], bias=self.cvap(f"bx{l}", ch))

    def lru_chain(self, st, l, first):
        Ls, ns = st.Ls, st.nseq
        cb = st.cellb
        c3 = lambda cell: self.seg(st, self.cf(st, cell))
        CH = (0, 1)
        cvc = lambda ch: 11 + ch
        rc = lambda ch: 13 + ch
        ic = lambda ch: 15 + ch
        ac = lambda ch: 2 + ch
        mc = lambda ch: 4 + ch
        hc = lambda ch: (18, 0)[ch]
        for ch in CH:
            cl = self.dv[:, 16 + 2 * l + ch:16 + 2 * l + ch + 1]
            cl2 = self.dv[:, 20 + 2 * l + ch:20 + 2 * l + ch + 1]
            self.act(self.cf(st, ac(ch)), self.cf(st, rc(ch)), AF.Exp, [cb[rc(ch)], self.dvb], [cb[ac(ch)]], scale=cl)
            self.act(self.cf(st, mc(ch)), self.cf(st, rc(ch)), AF.Exp, [cb[rc(ch)], self.dvb], [cb[mc(ch)]], scale=cl2)
        for ch in CH:
            self.tt(self.cf(st, ic(ch)), self.cf(st, ic(ch)), self.cf(st, cvc(ch)), ALU.mult, [cb[ic(ch)], cb[cvc(ch)]],
                    [cb[ic(ch)]], en=self.pen)
            self.ts(self.cf(st, mc(ch)), self.cf(st, mc(ch)), 0.99999994, None, ALU.min, None, [cb[mc(ch)]], [cb[mc(ch)]])
        for ch in CH:
            self.act(self.cf(st, mc(ch)), self.cf(st, mc(ch)), AF.Ln, [cb[mc(ch)]], [cb[mc(ch)]], scale=-1.0, bias=self.oneap)
        for ch in CH:
            self.act(self.cf(st, mc(ch)), self.cf(st, mc(ch)), AF.Exp, [cb[mc(ch)]], [cb[mc(ch)]], scale=0.5)
            if first:
                self.op("dve", lambda e: e.memset(st.arena[:, mc(ch), 0:1], 1.0), [], [cb[mc(ch)]])
        for ch in CH:
            self.tt(self.cf(st, ic(ch)), self.cf(st, ic(ch)), self.cf(st, mc(ch)), ALU.mult, [cb[ic(ch)], cb[mc(ch)]],
                    [cb[ic(ch)]])
        for ch in CH:
            for s in range(ns):
                sl = slice(s * Ls, (s + 1) * Ls)
                self.op("dve", lambda e: e.tensor_tensor_scan(
                    out=st.arena[:, hc(ch), sl], data0=st.arena[:, ac(ch), sl], data1=st.arena[:, ic(ch), sl],
                    initial=st.HL[:, l, ch, s:s + 1], op0=ALU.mult, op1=ALU.add),
                    [cb[ac(ch)], cb[ic(ch)], st.HLb[l]], [cb[hc(ch)]])
        for ch in CH:
            self.cp(st.HL[:, l, ch, :], c3(hc(ch))[:, :, Ls - 1], [cb[hc(ch)]], [st.HLb[l]])
            self.tt(st.ycat[:, 2 + ch, :], self.cf(st, hc(ch)), self.cf(st, 9 + ch), ALU.mult, [cb[hc(ch)], cb[9 + ch]],
                    [st.yb[2 + ch]], en=self.pen)

    def hg_prep(self, st, l):
        cb = st.cellb
        C, nch = st.C, st.nch
        pe = self.pen
        rm = self.cm[:, CM_RP:CM_RP + 512] if st.isP else self.cm[:, CM_RS:CM_RS + 64]
        H = range(4)
        cells = [(6, 7), (8, 17), (11, 13), (12, 14)]
        LF = lambda h: cells[h][0]
        B = lambda h: cells[h][1]
        sg = lambda h: 19 + h
        lb = lambda h: self.dv[:, 32 + 4 * l + h:32 + 4 * l + h + 1]
        oml = lambda h: self.dv[:, 8 + 4 * l + h:8 + 4 * l + h + 1]
        noml = lambda h: self.dv[:, 24 + 4 * l + h:24 + 4 * l + h + 1]
        for h in H:
            self.act(self.cf(st, LF(h)), self.cf(st, sg(h)), AF.Ln, [cb[sg(h)], self.dvb], [cb[LF(h)]],
                     scale=oml(h), bias=lb(h))
        for h in H:
            self.op("dve", lambda e: e.tensor_tensor_scan(out=self.cf(st, B(h)), data0=rm[:, 0:st.N],
                                                          data1=self.cf(st, LF(h)), initial=0.0,
                                                          op0=ALU.mult, op1=ALU.add),
                    [self.cmb, cb[LF(h)]], [cb[B(h)]])
            self.act(self.cf(st, sg(h)), self.cf(st, sg(h)), AF.Identity, [cb[sg(h)], self.dvb], [cb[sg(h)]],
                     scale=noml(h), bias=oml(h))
        for h in H:
            self.act(self.cf(st, LF(h)), self.cf(st, B(h)), AF.Exp, [cb[B(h)]], [cb[LF(h)]])
        for h in H:
            self.act(self.cf(st, B(h)), self.cf(st, B(h)), AF.Exp, [cb[B(h)]], [cb[B(h)]], scale=-1.0)
        for h in H:
            ebl_src = self.cf(st, LF(h)).rearrange("p (c j) -> p c j", j=C)[:, :, C - 1]
            self.cp(st.EBL[:, h, :], ebl_src, [cb[LF(h)]], [st.EBLb])
            self.tt(self.cb16(st, h // 2, h % 2), self.cf(st, 23 + h), self.cf(st, LF(h)), ALU.mult,
                    [cb[23 + h], cb[LF(h)]], [cb[h // 2]])
            self.tt(self.cb16(st, 2 + h // 2, h % 2), self.cf(st, sg(h)), self.cf(st, B(h)), ALU.mult,
                    [cb[sg(h)], cb[B(h)]], [cb[2 + h // 2]], en=(pe if h < 2 else "dve"))

    def hg_vtok_P(self, st, w, wb):
        for blk in range(4):
            VTp, VTpb = st.psVT[blk % 2]
            cols = slice(blk * 128, (blk + 1) * 128)
            self.mm(VTp[:, :], VTpb, [(st.h[:, k, cols], w[:, k * 512:(k + 1) * 512]) for k in range(NK)],
                    [wb] + st.hb)
            self.cp(st.VT[:, blk, :], VTp[:, :], [VTpb], [st.VTb[blk]], en="act")

    def hg_chunks_P(self, st, l):
        cb = st.cellb
        bm = self.cm[:, CM_BM:CM_BM + 128]
        S32, S32b = st.S32[l][0], st.S32b[l][0]
        S16, S16b = st.S16[l][0], st.S16b[l][0]

        def front(blk):
            par = blk % 2
            c128 = slice(blk * 128, (blk + 1) * 128)
            KTp, KTpb = st.psKTT
            self._wait("pe", self._deps("pe", [cb[2], cb[3], self.identb], [KTpb]))
            ins = None
            for h in range(4):
                ins = self.nc.tensor.transpose(out=KTp[:, h * 128:(h + 1) * 128],
                                               in_=self.cb16(st, 2 + h // 2, h % 2)[:, c128],
                                               identity=self.ident[:, :])
            self.ecnt["pe"] += 1
            ev = (self.esem["pe"], self.ecnt["pe"])
            ins.then_inc(ev[0], 1)
            KTpb.w = ev
            KTpb.r = {}
            for b_ in (cb[2], cb[3], self.identb):
                b_.r[ev[0].num] = ev
            self.cp(st.KTT[:, par, :], KTp[:, 0:512], [KTpb], [st.KTTb[par]])
            ATp, ATpb = st.psATT
            for h in range(4):
                self.mm(ATp[:, h * 128:(h + 1) * 128], ATpb,
                        [(self.cb16(st, 2 + h // 2, h % 2)[:, c128], self.cb16(st, h // 2, h % 2)[:, c128])],
                        [cb[2], cb[3], cb[0], cb[1]])
            self.tt(st.ATT[:, par, :].rearrange("p (h t) -> p h t", h=4),
                    ATp[:, :].rearrange("p (h t) -> p h t", h=4), bm.unsqueeze(1).broadcast_to([128, 4, 128]),
                    ALU.mult, [ATpb, self.cmb], [st.ATTb[par]])

        def back(blk):
            par = blk % 2
            for sub in range(2):
                c = 2 * blk + sub
                p0 = 64 * sub
                cols = slice(c * 64, (c + 1) * 64)
                OCp, OCpb = st.psOC[c % 2]
                for h in range(4):
                    self.mm(OCp[:, h * 64:(h + 1) * 64], OCpb,
                            [(st.VT[p0:p0 + 64, blk, h * 128:(h + 1) * 128],
                              st.ATT[p0:p0 + 64, par, h * 128 + p0:h * 128 + p0 + 64]),
                             (S16[:, h * 128:(h + 1) * 128], self.cb16(st, h // 2, h % 2)[:, cols])],
                            [st.VTb[blk], st.ATTb[par], S16b, cb[0], cb[1]])
                self.cp(st.arena[:, 19:23, c * 64:(c + 1) * 64], OCp[:, 0:256].rearrange("p (h t) -> p h t", h=4),
                        [OCpb], cb[19:23], en="act")
                KVp, KVpb = st.psKV
                for h in range(4):
                    self.mm(KVp[:, h * 128:(h + 1) * 128], KVpb,
                            [(st.KTT[p0:p0 + 64, par, h * 128:(h + 1) * 128], st.VT[p0:p0 + 64, blk, h * 128:(h + 1) * 128])],
                            [st.KTTb[par], st.VTb[blk]])
                s3 = S32[:, :].rearrange("p (h v) -> p h v", h=4)
                self.tt(S32[:, :], S32[:, :], KVp[:, :], ALU.add, [S32b, KVpb], [S32b])
                self.tt(s3, s3, st.EBL[:, :, c:c + 1].broadcast_to([128, 4, 128]), ALU.mult, [S32b, st.EBLb], [S32b])
                self.cp(S16[:, :], S32[:, :], [S32b], [S16b], en="act")

        front(0)
        for blk in range(4):
            if blk + 1 < 4:
                front(blk + 1)
            back(blk)

    def hg_chunks(self, st, l, w, wb):
        cb = st.cellb
        C, nch = st.C, st.nch
        ca = self.cm[0:C, CM_CA:CM_CA + C]
        for c in range(nch):
            par = c % 2
            cols = slice(c * C, (c + 1) * C)
            sidx = 0 if st.isP else c % 2
            S32, S32b = st.S32[l][sidx], st.S32b[l][sidx]
            S16, S16b = st.S16[l][sidx], st.S16b[l][sidx]
            if not st.isP:
                self.dma("sp", S32[:, :], self.d_shg[l, c], [], [S32b], self.misc())
                self.cp(S16[:, :], S32[:, :], [S32b], [S16b], en="act")
            VTp, VTpb = st.psVT[par]
            self.mm(VTp[0:C, :], VTpb, [(st.h[:, k, cols], w[:, k * 512:(k + 1) * 512]) for k in range(NK)],
                    [wb] + st.hb)
            self.cp(st.VT[0:C, par, :], VTp[0:C, :], [VTpb], [st.VTb[par]], en="act")
            KTp, KTpb = st.psKTT
            self._wait("pe", self._deps("pe", [cb[2], cb[3], self.identb], [KTpb]))
            ins = None
            for h in range(4):
                ins = self.nc.tensor.transpose(out=KTp[0:C, h * 128:(h + 1) * 128],
                                               in_=self.cb16(st, 2 + h // 2, h % 2)[:, cols],
                                               identity=self.ident[:, :])
            self.ecnt["pe"] += 1
            ev = (self.esem["pe"], self.ecnt["pe"])
            ins.then_inc(ev[0], 1)
            KTpb.w = ev
            KTpb.r = {}
            for b in (cb[2], cb[3], self.identb):
                b.r[ev[0].num] = ev
            self.cp(st.KTT[0:C, par, :], KTp[0:C, 0:512], [KTpb], [st.KTTb[par]])
            ATp, ATpb = st.psATT
            for h in range(4):
                self.mm(ATp[0:C, h * C:(h + 1) * C], ATpb,
                        [(self.cb16(st, 2 + h // 2, h % 2)[:, cols], self.cb16(st, h // 2, h % 2)[:, cols])],
                        [cb[2], cb[3], cb[0], cb[1]])
            att_out = st.ATT[0:C, par, 0:4 * C].rearrange("p (h t) -> p h t", h=4)
            att_in = ATp[0:C, 0:4 * C].rearrange("p (h t) -> p h t", h=4)
            self.tt(att_out, att_in, ca.unsqueeze(1).broadcast_to([C, 4, C]), ALU.mult, [ATpb, self.cmb],
                    [st.ATTb[par]])
            OCp, OCpb = st.psOC[par]
            for h in range(4):
                self.mm(OCp[:, h * C:(h + 1) * C], OCpb,
                        [(st.VT[0:C, par, h * 128:(h + 1) * 128], st.ATT[0:C, par, h * C:(h + 1) * C]),
                         (S16[:, h * 128:(h + 1) * 128], self.cb16(st, h // 2, h % 2)[:, cols])],
                        [st.VTb[par], st.ATTb[par], S16b, cb[0], cb[1]])
            os_out = st.arena[:, 19:23, c * C:(c + 1) * C]
            self.cp(os_out, OCp[:, 0:4 * C].rearrange("p (h t) -> p h t", h=4), [OCpb], cb[19:23], en="act")
            KVp, KVpb = st.psKV
            for h in range(4):
                self.mm(KVp[:, h * 128:(h + 1) * 128], KVpb,
                        [(st.KTT[0:C, par, h * 128:(h + 1) * 128], st.VT[0:C, par, h * 128:(h + 1) * 128])],
                        [st.KTTb[par], st.VTb[par]])
            s3 = S32[:, :].rearrange("p (h v) -> p h v", h=4)
            self.tt(S32[:, :], S32[:, :], KVp[:, :], ALU.add, [S32b, KVpb], [S32b])
            self.tt(s3, s3, st.EBL[:, :, c:c + 1].broadcast_to([128, 4, 128]), ALU.mult, [S32b, st.EBLb], [S32b])
            if st.isP:
                self.cp(S16[:, :], S32[:, :], [S32b], [S16b], en="act")
            else:
                self.dma("sp", self.d_hgS[l, c], S32[:, :], [S32b], [], self.misc(), is_output=True)

    def hg_out(self, st, l):
        cb = st.cellb
        H = range(4)
        oc = lambda h: 19 + h
        o2 = lambda h: self.cb16(st, 6 + h // 2, h % 2)
        o2b = lambda h: cb[6 + h // 2]
        rc = lambda h: 11 + h
        banks = [st.psSS, st.psVT[0], st.psSS, st.psVT[0]] if st.isP else [st.psSS] * 4
        for h in H:
            self.act(o2(h), self.cf(st, oc(h)), AF.Square, [cb[oc(h)]], [o2b(h)])
        for pair in (((0, 1), (2, 3)) if st.isP else ((0,), (1,), (2,), (3,))):
            for h in pair:
                ss, ssb = banks[h]
                self.mm(ss[:, 0:st.N], ssb, [(self.ones[:, :], o2(h))], [self.onesb, o2b(h)])
            for h in pair:
                ss, ssb = banks[h]
                self.act(self.cf(st, rc(h)), ss[:, 0:st.N], AF.Ln, [ssb, self.epsb], [cb[rc(h)]], scale=1.0 / 128,
                         bias=self.epsap)
        for h in H:
            self.act(self.cf(st, rc(h)), self.cf(st, rc(h)), AF.Exp, [cb[rc(h)]], [cb[rc(h)]], scale=-0.5)
        for h in H:
            self.stt(self.cf(st, rc(h)), self.cf(st, oc(h)), self.cvap(f"hgn{l}", h), self.cf(st, rc(h)), ALU.mult,
                     ALU.mult, [cb[oc(h)], self.cvb, cb[rc(h)]], [cb[rc(h)]])
        for h in H:
            self.tt(st.ycat[:, 4 + h, :], self.cf(st, rc(h)), self.cb16(st, 27 + h // 2, h % 2), ALU.mult,
                    [cb[rc(h)], cb[27 + h // 2]], [st.yb[4 + h]], en=(self.pen if h % 2 else "dve"))

    def final(self, st, dst):
        self.norm(st, st.xi, "fn", final=True)
        x = st.xres[st.xi]
        for k in range(NK):
            cell = 4 + k
            self.stt(self.cf(st, cell), x[:, k, :], self.cvap("fn", k), st.rstd[:, :], ALU.mult, ALU.mult,
                     [st.xb[st.xi][k], self.cvb, st.rstdb], [st.cellb[cell]])
            self.dma("sp", dst[k * 128:(k + 1) * 128, :], self.cf(st, cell), [st.cellb[cell]], [], self.ysem[k],
                     is_output=True)

    def build(self):
        nc = self.nc
        self.setup()
        self.eps_t = nc.alloc_sbuf_tensor("eps_t", [128, 1], F32)
        self.epsb = Buf()
        self.epsap = self.eps_t[:, 0:1]
        self.op("dve", lambda e: e.memset(self.eps_t[:, :], EPS), [], [self.epsb])
        self.one_t = nc.alloc_sbuf_tensor("one_t", [128, 1], F32)
        self.oneap = self.one_t[:, 0:1]
        self.op("dve", lambda e: e.memset(self.one_t[:, :], 1.0), [], [self.epsb])
        self.prologue()
        P, S = self.P, self.S
        self.load_x(0)
        for i in range(self.NT):
            P.xi = i % 2
            self.pen = "dve" if i == 0 else "pool"
            if i + 1 < self.NT:
                self.load_x(i + 1)
            streams = [P]
            if i == 0 and S is not None:
                S.xi = 0
                streams = [P, S]
            for l in range(NL):
                self.ffn(streams, l, 0)
                self.mixer(streams, l, first=(i == 0))
                self.ffn(streams, l, 1)
            self.final(P, self.d_yp[:, i * TP:(i + 1) * TP])
            if i == 0 and S is not None:
                self.final(S, self.d_ys[:, :])
                self.dma("sp", self.d_poolS[:, :], S.UH[:].rearrange("p l c s j -> p (l c s j)"), S.UHb, [],
                         self.misc(), is_output=True)
                self.dma("sp", self.d_convS[:, :], S.XH[:].rearrange("p l c s j -> p (l c s j)"), S.XHb, [],
                         self.misc(), is_output=True)
                self.dma("sp", self.d_lruS[:, :], S.HL[:].rearrange("p l c s -> p (l c s)"), S.HLb, [],
                         self.misc(), is_output=True)
        for l in range(NL):
            self.dma("sp", self.d_poolP[l], P.UH[:, l].rearrange("p c s j -> p (c s j)"), [P.UHb[l]], [],
                     self.misc(), is_output=True)
            self.dma("sp", self.d_convP[l], P.XH[:, l].rearrange("p c s j -> p (c s j)"), [P.XHb[l]], [],
                     self.misc(), is_output=True)
            self.dma("sp", self.d_lruP[l], P.HL[:, l].rearrange("p c s -> p (c s)"), [P.HLb[l]], [],
                     self.misc(), is_output=True)
            self.dma("sp", self.d_hgP[l], P.S32[l][0][:, :], [P.S32b[l][0]], [], self.misc(), is_output=True)
        self._wait("sp", self.out_events)
        return nc


def prep_inputs(inp, n_cores, NT, with_sample=True):
    T = NT * TP
    wpk = pack_weights(inp)
    cv, cm, bd = pack_consts(inp)
    maps = []
    for c in range(n_cores):
        m = {"wpk": wpk, "cv": cv, "cm": cm, "bd": bd,
             "xp": np.ascontiguousarray(inp["x_prompt"][c, :T].T)}
        if with_sample:
            sl = slice(NSEQ_S * c, NSEQ_S * c + NSEQ_S)
            m["xs"] = np.ascontiguousarray(inp["x_sample"][sl].reshape(NSEQ_S * LS, D).T)
            sp = inp["state_pool"][:, sl].reshape(NL, NSEQ_S, 15, 2, 128).transpose(4, 0, 3, 1, 2)
            m["spool"] = np.ascontiguousarray(sp).reshape(128, NL * 120)
            sc = inp["state_conv"][:, sl].reshape(NL, NSEQ_S, 3, 2, 128).transpose(4, 0, 3, 1, 2)
            m["sconv"] = np.ascontiguousarray(sc).reshape(128, NL * 24)
            slr = inp["state_lru"][:, sl].reshape(NL, NSEQ_S, 2, 128).transpose(3, 0, 2, 1)
            m["slru"] = np.ascontiguousarray(slr).reshape(128, NL * 8)
            sh = inp["state_hgrn"][:, sl].transpose(0, 1, 3, 2, 4)
            m["shg"] = np.ascontiguousarray(sh).reshape(NL, NSEQ_S, 128, 512)
        maps.append(m)
    return maps


def assemble(results, n_cores, NT, with_sample=True):
    T = NT * TP
    B = n_cores
    y_p = np.empty((B, T, D), np.float32)
    pool_p = np.empty((NL, B, 15, 256), np.float32)
    conv_p = np.empty((NL, B, 3, 256), np.float32)
    lru_p = np.empty((NL, B, 256), np.float32)
    hg_p = np.empty((NL, B, 4, 128, 128), np.float32)
    BS = NSEQ_S * n_cores
    y_s = np.empty((BS, LS, D), np.float32)
    pool_s = np.empty((NL, BS, 15, 256), np.float32)
    conv_s = np.empty((NL, BS, 3, 256), np.float32)
    lru_s = np.empty((NL, BS, 256), np.float32)
    hg_s = np.empty((NL, BS, 4, 128, 128), np.float32)
    for c in range(n_cores):
        r = results[c]
        y_p[c] = r["yp"].T
        pool_p[:, c] = r["poolP"].reshape(NL, 128, 2, 15).transpose(0, 3, 2, 1).reshape(NL, 15, 256)
        conv_p[:, c] = r["convP"].reshape(NL, 128, 2, 3).transpose(0, 3, 2, 1).reshape(NL, 3, 256)
        lru_p[:, c] = r["lruP"].reshape(NL, 128, 2).transpose(0, 2, 1).reshape(NL, 256)
        hg_p[:, c] = r["hgP"].reshape(NL, 128, 4, 128).transpose(0, 2, 1, 3)
        if with_sample:
            sl = slice(NSEQ_S * c, NSEQ_S * c + NSEQ_S)
            y_s[sl] = r["ys"].T.reshape(NSEQ_S, LS, D)
            ps = r["poolS"].reshape(128, NL, 2, NSEQ_S, 15).transpose(1, 3, 4, 2, 0)
            pool_s[:, sl] = ps.reshape(NL, NSEQ_S, 15, 256)
            cs = r["convS"].reshape(128, NL, 2, NSEQ_S, 3).transpose(1, 3, 4, 2, 0)
            conv_s[:, sl] = cs.reshape(NL, NSEQ_S, 3, 256)
            ls = r["lruS"].reshape(128, NL, 2, NSEQ_S).transpose(1, 3, 2, 0)
            lru_s[:, sl] = ls.reshape(NL, NSEQ_S, 256)
            hs = r["hgS"].reshape(NL, NSEQ_S, 128, 4, 128).transpose(0, 1, 3, 2, 4)
            hg_s[:, sl] = hs
    return (y_p, y_s, pool_p, conv_p, lru_p, hg_p, pool_s, conv_s, lru_s, hg_s)


def run(inp, n_cores=8, NT=16, with_sample=True):
    inp = {k: np.asarray(v) for k, v in inp.items()}
    nc = KB(NT, with_sample).build()
    maps = prep_inputs(inp, n_cores, NT, with_sample)
    res = run_bass_kernel_spmd(nc, maps, core_ids=list(range(n_cores)))
    return assemble(res.results, n_cores, NT, with_sample)


def kernel(**inputs):
    return run(inputs, 8, 16, True)
```
